# Optimizing a Trainium2 kernel written in Bass

```python
import math
import jax, jax.numpy as jnp
from jax import lax
import numpy as np

D_MODEL = 2048
BATCH = 4
SEQ = 2048
DEPTH = 4

HEAD_DIM_A = 128
N_HEADS_A = D_MODEL // (2 * HEAD_DIM_A)
WIDTH_A = N_HEADS_A * HEAD_DIM_A
CONV_K = 5
CHUNK = 64
WIDTH_B = D_MODEL // 2
S5_GROUP_CH = 16
N_GROUPS_B = WIDTH_B // S5_GROUP_CH
S5_STATE = 64
DT_MIN = 0.001
DT_MAX = 0.1
RMS_EPS = 1e-6

COL_QKV = 3 * WIDTH_A
COL_ZA = WIDTH_A
COL_BETA = 2 * N_HEADS_A
COL_ALPHA = 2 * N_HEADS_A
COL_U = WIDTH_B
COL_ZB = WIDTH_B
COL_GATES = 2 * D_MODEL
PROJ_WIDTH = COL_QKV + COL_ZA + COL_BETA + COL_ALPHA + COL_U + COL_ZB + COL_GATES
SPLIT_POINTS = list(np.cumsum([COL_QKV, COL_ZA, COL_BETA, COL_ALPHA, COL_U, COL_ZB]).tolist())

kernel_name = "hybrid_gdn_s5_bidir_encoder"


def rmsnorm(x, g):
    xf = x.astype(jnp.float32)
    y = xf * lax.rsqrt(jnp.mean(xf * xf, axis=-1, keepdims=True) + RMS_EPS)
    return (y * g.astype(jnp.float32)).astype(x.dtype)


def l2norm(x):
    xf = x.astype(jnp.float32)
    return xf * lax.rsqrt(jnp.sum(xf * xf, axis=-1, keepdims=True) + RMS_EPS)


def centred_dwconv(x, w):
    pad = (CONV_K - 1) // 2
    L = x.shape[1]
    xp = jnp.pad(x, ((0, 0), (pad, pad), (0, 0)))
    return sum(xp[:, i:i + L] * w[i] for i in range(CONV_K))


def _to_chunks(t):
    b, l, h = t.shape[:3]
    t = t.reshape((b, l // CHUNK, CHUNK, h) + t.shape[3:])
    return jnp.moveaxis(jnp.moveaxis(t, 1, 0), 3, 2)


def gated_delta_chunked(q, k, v, g, beta):
    b_, l_, h_, dv = v.shape
    qc, kc, vc = _to_chunks(q), _to_chunks(k), _to_chunks(v)
    gc = jnp.cumsum(_to_chunks(g), axis=-1)
    bc = _to_chunks(beta)
    idx = jnp.arange(CHUNK)
    incl = idx[:, None] >= idx[None, :]
    strict = idx[:, None] > idx[None, :]
    decay = jnp.exp(jnp.where(incl, gc[..., :, None] - gc[..., None, :], -jnp.inf))
    kb = kc * bc[..., None]
    vb = vc * bc[..., None]
    lmat = jnp.where(strict, jnp.einsum('nbhck,nbhsk->nbhcs', kb, kc) * decay, 0.0)
    u = lax.linalg.triangular_solve(lmat, vb, left_side=True, lower=True, unit_diagonal=True)
    w = lax.linalg.triangular_solve(lmat, kb * jnp.exp(gc)[..., None], left_side=True, lower=True, unit_diagonal=True)
    qk = jnp.einsum('nbhck,nbhsk->nbhcs', qc, kc) * decay

    def step(S, xs):
        q_c, k_c, u_c, w_c, g_c, qk_c = xs
        v_new = u_c - jnp.einsum('bhck,bhkv->bhcv', w_c, S)
        o_c = (jnp.einsum('bhck,bhkv->bhcv', q_c * jnp.exp(g_c)[..., None], S)
               + jnp.einsum('bhcs,bhsv->bhcv', qk_c, v_new))
        g_last = g_c[..., -1:]
        S = (S * jnp.exp(g_last)[..., None]
             + jnp.einsum('bhck,bhcv->bhkv', k_c * jnp.exp(g_last - g_c)[..., None], v_new))
        return S, o_c

    s0 = jnp.zeros((b_, h_, q.shape[-1], dv), jnp.float32)
    _, o = lax.scan(step, s0, (qc, kc, u, w, gc, qk))
    o = jnp.moveaxis(jnp.moveaxis(o, 2, 3), 0, 1)
    return o.reshape(b_, l_, h_, dv)


def bidir_gated_delta(q, k, v, g, beta):
    flip = lambda t: jnp.flip(t, axis=1)
    fwd = gated_delta_chunked(q, k, v, g[:, :, 0], beta[:, :, 0])
    bwd = flip(gated_delta_chunked(flip(q), flip(k), flip(v), flip(g[:, :, 1]), flip(beta[:, :, 1])))
    return fwd + bwd


def _ssm_combine(left, right):
    a_i, b_i = left
    a_j, b_j = right
    return a_j * a_i, a_j * b_i + b_j


def s5_bidirectional(u, lam_re, lam_im, log_dt, b_re, b_im, c_re, c_im, d_skip):
    f32 = jnp.float32
    bsz, L, _ = u.shape
    ug = u.astype(f32).reshape(bsz, L, N_GROUPS_B, S5_GROUP_CH)
    ugc = ug.astype(jnp.complex64)
    lam = lax.complex(lam_re.astype(f32), lam_im.astype(f32))
    dt = jnp.exp(log_dt.astype(f32))[..., None]
    lam_bar = jnp.exp(lam * dt)
    b_bar = ((lam_bar - 1.0) / lam)[..., None] * lax.complex(b_re.astype(f32), b_im.astype(f32))
    c = lax.complex(c_re.astype(f32), c_im.astype(f32))

    def one_direction(d, reverse):
        bu = jnp.einsum('gpc,blgc->blgp', b_bar[d], ugc)
        a = jnp.broadcast_to(lam_bar[d], bu.shape)
        _, states = lax.associative_scan(_ssm_combine, (a, bu), axis=1, reverse=reverse)
        return jnp.einsum('gcp,blgp->blgc', c[d], states).real

    y = (one_direction(0, False) + one_direction(1, True)
         + ug * d_skip.astype(f32).reshape(N_GROUPS_B, S5_GROUP_CH))
    return y.reshape(bsz, L, WIDTH_B).astype(u.dtype)


def hybrid_layer(x, ln_g, w_in, conv_w, a_log, dt_bias, head_norm_g, lam_re, lam_im, log_dt,
                 b_re, b_im, c_re, c_im, d_skip, w_glu, b_glu, w_pa, w_pb, b_gate, w_out):
    bsz, L, _ = x.shape
    h = rmsnorm(x, ln_g)
    proj = h @ w_in
    qkv, z_a, beta_logit, alpha_logit, u, z_b, gate_logit = jnp.split(proj, SPLIT_POINTS, axis=-1)

    qkv = jax.nn.silu(centred_dwconv(qkv, conv_w))
    q, k, v = jnp.split(qkv, 3, axis=-1)
    q = l2norm(q.reshape(bsz, L, N_HEADS_A, HEAD_DIM_A)) * (HEAD_DIM_A ** -0.5)
    k = l2norm(k.reshape(bsz, L, N_HEADS_A, HEAD_DIM_A))
    v = v.reshape(bsz, L, N_HEADS_A, HEAD_DIM_A).astype(jnp.float32)
    beta = jax.nn.sigmoid(beta_logit.astype(jnp.float32).reshape(bsz, L, 2, N_HEADS_A))
    g = -jnp.exp(a_log.astype(jnp.float32)) * jax.nn.softplus(
        alpha_logit.astype(jnp.float32).reshape(bsz, L, 2, N_HEADS_A) + dt_bias.astype(jnp.float32))
    o_a = bidir_gated_delta(q, k, v, g, beta)
    o_a = rmsnorm(o_a, head_norm_g).reshape(bsz, L, WIDTH_A).astype(x.dtype)
    y_a = (o_a * jax.nn.silu(z_a)) @ w_pa

    y_s = jax.nn.gelu(s5_bidirectional(u, lam_re, lam_im, log_dt, b_re, b_im, c_re, c_im, d_skip))
    y_s = y_s * jax.nn.sigmoid(y_s @ w_glu + b_glu)
    y_b = (y_s * jax.nn.silu(z_b)) @ w_pb

    gate_a, gate_b = jnp.split(jax.nn.sigmoid(gate_logit + b_gate), 2, axis=-1)
    merged = gate_a * y_a + gate_b * y_b
    return x + merged @ w_out


def setup_inputs(seed: int = 0) -> dict:
    key = jax.random.key(seed)
    ks = jax.random.split(key, 24)
    f32 = jnp.float32
    nrm = lambda k, shape, scale: scale * jax.random.normal(k, shape, f32)
    x = jax.random.normal(ks[0], (BATCH, SEQ, D_MODEL), f32)
    ln_g = 1.0 + nrm(ks[1], (DEPTH, D_MODEL), 0.02)
    w_in = nrm(ks[2], (DEPTH, D_MODEL, PROJ_WIDTH), D_MODEL ** -0.5)
    conv_w = nrm(ks[3], (DEPTH, CONV_K, 3 * WIDTH_A), CONV_K ** -0.5)
    a_log = jnp.log(jax.random.uniform(ks[4], (DEPTH, 2, N_HEADS_A), f32, 1.0, 16.0))
    dt = jnp.exp(jax.random.uniform(ks[5], (DEPTH, 2, N_HEADS_A), f32, math.log(DT_MIN), math.log(DT_MAX)))
    dt_bias = dt + jnp.log(-jnp.expm1(-dt))
    head_norm_g = 1.0 + nrm(ks[6], (DEPTH, HEAD_DIM_A), 0.02)
    ssm_shape = (DEPTH, 2, N_GROUPS_B, S5_STATE)
    lam_re = -0.5 + nrm(ks[7], ssm_shape, 0.01)
    lam_im = jnp.pi * jnp.arange(S5_STATE, dtype=f32) + nrm(ks[8], ssm_shape, 0.01)
    log_dt = jax.random.uniform(ks[9], (DEPTH, 2, N_GROUPS_B), f32, math.log(DT_MIN), math.log(DT_MAX))
    b_shape = (DEPTH, 2, N_GROUPS_B, S5_STATE, S5_GROUP_CH)
    b_re = nrm(ks[10], b_shape, (2 * S5_GROUP_CH) ** -0.5)
    b_im = nrm(ks[11], b_shape, (2 * S5_GROUP_CH) ** -0.5)
    c_shape = (DEPTH, 2, N_GROUPS_B, S5_GROUP_CH, S5_STATE)
    c_re = nrm(ks[12], c_shape, (2 * S5_STATE) ** -0.5)
    c_im = nrm(ks[13], c_shape, (2 * S5_STATE) ** -0.5)
    d_skip = nrm(ks[14], (DEPTH, WIDTH_B), 1.0)
    w_glu = nrm(ks[15], (DEPTH, WIDTH_B, WIDTH_B), WIDTH_B ** -0.5)
    b_glu = nrm(ks[16], (DEPTH, WIDTH_B), 0.01)
    w_pa = nrm(ks[17], (DEPTH, WIDTH_A, D_MODEL), WIDTH_A ** -0.5)
    w_pb = nrm(ks[18], (DEPTH, WIDTH_B, D_MODEL), WIDTH_B ** -0.5)
    b_gate = nrm(ks[19], (DEPTH, 2 * D_MODEL), 0.01)
    w_out = nrm(ks[20], (DEPTH, D_MODEL, D_MODEL), D_MODEL ** -0.5)
    final_g = 1.0 + nrm(ks[21], (D_MODEL,), 0.02)
    return {"x": x, "ln_g": ln_g, "w_in": w_in, "conv_w": conv_w, "a_log": a_log,
            "dt_bias": dt_bias, "head_norm_g": head_norm_g, "lam_re": lam_re, "lam_im": lam_im,
            "log_dt": log_dt, "b_re": b_re, "b_im": b_im, "c_re": c_re, "c_im": c_im,
            "d_skip": d_skip, "w_glu": w_glu, "b_glu": b_glu, "w_pa": w_pa, "w_pb": w_pb,
            "b_gate": b_gate, "w_out": w_out, "final_g": final_g}


def reference(x, ln_g, w_in, conv_w, a_log, dt_bias, head_norm_g, lam_re, lam_im, log_dt,
              b_re, b_im, c_re, c_im, d_skip, w_glu, b_glu, w_pa, w_pb, b_gate, w_out, final_g):
    for l in range(DEPTH):
        x = hybrid_layer(x, ln_g[l], w_in[l], conv_w[l], a_log[l], dt_bias[l], head_norm_g[l],
                         lam_re[l], lam_im[l], log_dt[l], b_re[l], b_im[l], c_re[l], c_im[l],
                         d_skip[l], w_glu[l], b_glu[l], w_pa[l], w_pb[l], b_gate[l], w_out[l])
    return rmsnorm(x, final_g)
```

```python
import math
import numpy as np
import concourse.bass as bass
import concourse.mybir as mybir
from concourse.bass_utils import run_bass_kernel_spmd

F32 = mybir.dt.float32
BF16 = mybir.dt.bfloat16
I32 = mybir.dt.int32
AF = mybir.ActivationFunctionType
ALU = mybir.AluOpType
AX = mybir.AxisListType

T = 2048
D = 2048
NT = 16
DEPTH = 4
PROJ = 10272
EPS = 1e-6
NCORES = 8
TWO_PI = 2.0 * math.pi


class Buf:
    __slots__ = ("name", "lw", "rd")

    def __init__(self, name=""):
        self.name = name
        self.lw = None
        self.rd = []


class Prog:
    ENG = ("pe", "act", "dve", "pool", "sp")
    NDMA = {"sp": 8, "pool": 4}

    def __init__(self, nc, same_engine_sync=True):
        self.nc = nc
        self.ops = {e: [] for e in self.ENG}
        self.cnt = {e: 0 for e in self.ENG}
        self.known = {e: {} for e in self.ENG}
        self.sems = {}
        self.dma_n = {q: 0 for q in self.NDMA}
        self.dma_last = {}
        self.same = same_engine_sync
        self.epoch = 0
        for e in ("pe", "act", "dve", "pool"):
            self.sems[("c", e, 0)] = nc.alloc_semaphore("s_" + e)
        for q, n in self.NDMA.items():
            for i in range(n):
                self.sems[("d", q, i)] = nc.alloc_semaphore(f"d_{q}{i}")

    def _deps(self, reads, writes):
        deps = {}

        def add(d):
            if d is None:
                return
            k, v = d
            if deps.get(k, 0) < v:
                deps[k] = v
        for b in reads:
            add(b.lw)
        for b in writes:
            add(b.lw)
            for r in b.rd:
                add(r)
        return deps

    def _emit_waits(self, eng, deps):
        kn = self.known[eng]
        lst = []
        for k, v in deps.items():
            if not self.same and k[0] == "c" and k[1] == eng:
                continue
            if kn.get(k, 0) >= v:
                continue
            kn[k] = v
            lst.append((self.sems[k], v))
        return lst

    def _record(self, done, reads, writes):
        for b in reads:
            b.rd.append(done)
        for b in writes:
            b.lw = done
            b.rd = []

    def op(self, eng, fn, reads=(), writes=()):
        deps = self._deps(reads, writes)
        waits = self._emit_waits(eng, deps)
        self.cnt[eng] += 1
        done = (("c", eng, self.epoch), self.cnt[eng])
        sem = self.sems[("c", eng, self.epoch)]

        def emit(e, fn=fn, waits=waits, sem=sem):
            for s, v in waits:
                e.wait_ge(s, v)
            fn(e).then_inc(sem, 1)
        self.ops[eng].append(emit)
        self._record(done, reads, writes)
        return done

    def dma(self, out, in_, reads=(), writes=(), q="sp", **kw):
        n = self.dma_n[q]
        self.dma_n[q] += 1
        K = self.NDMA[q]
        key = ("d", q, n % K)
        gen = n // K
        deps = self._deps(reads, writes)
        if gen > 0 and deps.get(key, 0) < 16 * gen:
            deps[key] = 16 * gen
        waits = self._emit_waits(q, deps)
        done = (key, 16 * (gen + 1))
        self.dma_last[key] = 16 * (gen + 1)
        sem = self.sems[key]

        def emit(e, waits=waits, sem=sem, out=out, in_=in_, kw=kw):
            for s, v in waits:
                e.wait_ge(s, v)
            e.dma_start(out=out, in_=in_, **kw).then_inc(sem, 16)
        self.ops[q].append(emit)
        self._record(done, reads, writes)
        return done

    def barrier(self):
        cur = {}
        for e in ("pe", "act", "dve", "pool"):
            if self.cnt[e] > 0:
                cur[("c", e, self.epoch)] = self.cnt[e]
        for k, v in self.dma_last.items():
            cur[k] = v
        for eng in self.ENG:
            waits = self._emit_waits(eng, dict(cur))

            def emit(e, waits=waits):
                for s, v in waits:
                    e.wait_ge(s, v)
            self.ops[eng].append(emit)

    def new_epoch(self):
        self.barrier()
        self.epoch += 1
        for e in ("pe", "act", "dve", "pool"):
            self.sems[("c", e, self.epoch)] = self.nc.alloc_semaphore(f"s_{e}{self.epoch}")
            self.cnt[e] = 0

    def build(self):
        nc = self.nc
        ops = self.ops
        with nc.Block() as block:
            @block.sync
            def _(e):
                for f in ops["sp"]:
                    f(e)

            @block.tensor
            def _(e):
                for f in ops["pe"]:
                    f(e)

            @block.scalar
            def _(e):
                for f in ops["act"]:
                    f(e)

            @block.vector
            def _(e):
                for f in ops["dve"]:
                    f(e)

            @block.gpsimd
            def _(e):
                for f in ops["pool"]:
                    f(e)


class Arena:
    def __init__(self, t, words):
        self.t = t
        self.words = words
        self.off = 0

    def reset(self):
        self.off = 0

    def alloc(self, shape, dtype, name=""):
        n = 1
        for s in shape:
            n *= s
        nbytes = n * (2 if dtype == BF16 else 4)
        w = (nbytes + 3) // 4
        w = (w + 7) // 8 * 8
        assert self.off + w <= self.words, f"arena overflow {name} {self.off + w} > {self.words}"
        ap = self.t[:, self.off:self.off + w]
        self.off += w
        if dtype != F32:
            ap = ap.bitcast(dtype)
        ap = ap[:, :n]
        if len(shape) == 2:
            ap = ap.rearrange("p (a b) -> p a b", a=shape[0], b=shape[1])
        elif len(shape) == 3:
            ap = ap.rearrange("p (a b c) -> p a b c", a=shape[0], b=shape[1], c=shape[2])
        return ap


class TB:
    __slots__ = ("ap", "b")

    def __init__(self, ap, name=""):
        self.ap = ap
        self.b = Buf(name)


AR_WORDS = 42000


def build_program(stop_after=None, debug=False, nlayers=DEPTH):
    nc = bass.Bass("TRN2", target_bir_lowering=False)
    p = Prog(nc)
    okind = "ExternalOutput" if debug else "Internal"

    def din(name, shape):
        return nc.dram_tensor(name, list(shape), F32, kind="ExternalInput").ap()

    x_in = din("x", [T, D])
    ln_g = din("ln_g", [DEPTH, D])
    w_in = din("w_in", [DEPTH, D, PROJ])
    conv_w = din("conv_w", [DEPTH, 5, 3072])
    a_log = din("a_log", [DEPTH, 16])
    dt_bias = din("dt_bias", [DEPTH, 16])
    head_norm_g = din("head_norm_g", [DEPTH, 128])
    lam_re = din("lam_re", [DEPTH, 128, 64])
    lam_im = din("lam_im", [DEPTH, 128, 64])
    log_dt = din("log_dt", [DEPTH, 128])
    b_re = din("b_re", [DEPTH, 2, 64, 64, 16])
    b_im = din("b_im", [DEPTH, 2, 64, 64, 16])
    c_re = din("c_re", [DEPTH, 2, 64, 16, 64])
    c_im = din("c_im", [DEPTH, 2, 64, 16, 64])
    d_skip = din("d_skip", [DEPTH, 1024])
    w_glu = din("w_glu", [DEPTH, 1024, 1024])
    b_glu = din("b_glu", [DEPTH, 1024])
    w_pa = din("w_pa", [DEPTH, 1024, D])
    w_pb = din("w_pb", [DEPTH, 1024, D])
    b_gate = din("b_gate", [DEPTH, 4096])
    w_out = din("w_out", [DEPTH, D, D])
    final_g = din("final_g", [D])
    out_d = nc.dram_tensor("out", [T, D], F32, kind="ExternalOutput").ap()

    def dscr(name, shape, dt):
        t = nc.dram_tensor(name, list(shape), dt, kind=okind).ap()
        return TB(t, name)

    XRES = dscr("XRES", [T, D], F32)
    QKV = dscr("QKV", [3072, T], BF16)
    ZA = dscr("ZA", [1024, T], BF16)
    UU = dscr("UU", [1024, T], BF16)
    ZB = dscr("ZB", [1024, T], BF16)
    GT = dscr("GT", [4096, T], BF16)
    QN = dscr("QN", [1024, T], BF16)
    KN = dscr("KN", [1024, T], BF16)
    KTM = dscr("KTM", [8, 128, 16, 128], BF16)
    VTM = dscr("VTM", [8, 128, 16, 128], BF16)
    OAZ = dscr("OAZ", [1024, T], BF16)
    YBZ = dscr("YBZ", [1024, T], BF16)
    MRG = dscr("MRG", [D, T], BF16)
    OUTB = Buf("out")
    dbg = {}
    if debug:
        dbg["BA"] = dscr("BA", [128, 16, 32], F32)
        dbg["ORAW"] = dscr("ORAW", [8, 128, 16, 128], F32)
        dbg["S5Y"] = dscr("S5Y", [1024, T], F32)
        dbg["YSD"] = dscr("YSD", [1024, T], BF16)

    def sb(name, shape, dt):
        return TB(nc.alloc_sbuf_tensor(name, list(shape), dt)[:], name)

    ident_f = sb("ident_f", [128, 128], F32)
    ident_b = sb("ident_b", [128, 128], BF16)
    ones_f = sb("ones_f", [128, 128], F32)
    swap_f = sb("swap_f", [128, 128], F32)
    m_gt = sb("m_gt", [128, 128], F32)
    m_lt = sb("m_lt", [128, 128], F32)
    tri_le = sb("tri_le", [128, 128], F32)
    tri_ge = sb("tri_ge", [128, 128], F32)
    neg_s_f = sb("neg_s_f", [128, 128], F32)
    neg_i_f = sb("neg_i_f", [128, 128], F32)
    neg_s_b = sb("neg_s_b", [128, 128], F32)
    neg_i_b = sb("neg_i_b", [128, 128], F32)
    blkmask = sb("blkmask", [128, 8], F32)
    colmask = sb("colmask", [128, 8, 128], BF16)
    bg_sb = sb("bg_sb", [128, 128], F32)
    bglu_sb = sb("bglu_sb", [128, 32], F32)
    dsk_sb = sb("dsk_sb", [128, 32], F32)
    hng_sb = sb("hng_sb", [128, 4], F32)
    cw_sb = sb("cw_sb", [128, 480], F32)
    II = sb("II", [128, 256], BF16)
    lvmask = [[sb(f"lvm{d}_{lv}", [128, 128], BF16) for lv in range(7)] for d in range(2)]
    arena_t = nc.alloc_sbuf_tensor("arena", [128, AR_WORDS], F32)
    ar = Arena(arena_t[:], AR_WORDS)
    psb = [TB(nc.alloc_psum_tensor(f"ps{i}", [128, 512], F32)[:], f"ps{i}") for i in range(8)]

    V = lambda e: e

    def memset(eng, tb, val):
        p.op(eng, lambda e: e.memset(tb.ap, val), writes=[tb.b])

    def aff(tb, cmp, fill, base, cm, pat):
        p.op("pool", lambda e: e.affine_select(out=tb.ap, in_=tb.ap, compare_op=cmp, fill=fill, base=base,
                                               pattern=pat, channel_multiplier=cm),
             reads=[tb.b], writes=[tb.b])

    memset("pool", ident_f, 1.0)
    aff(ident_f, ALU.is_equal, 0.0, 0, 1, [[-1, 128]])
    p.op("pool", lambda e: e.tensor_copy(out=ident_b.ap, in_=ident_f.ap), reads=[ident_f.b], writes=[ident_b.b])
    memset("pool", ones_f, 1.0)
    swap2 = sb("swap2", [128, 128], F32)
    memset("pool", swap_f, 1.0)
    aff(swap_f, ALU.is_equal, 0.0, 64, 1, [[-1, 128]])
    memset("pool", swap2, 1.0)
    aff(swap2, ALU.is_equal, 0.0, -64, 1, [[-1, 128]])
    p.op("pool", lambda e: e.tensor_tensor(out=swap_f.ap, in0=swap_f.ap, in1=swap2.ap, op=ALU.add),
         reads=[swap_f.b, swap2.b], writes=[swap_f.b])
    memset("pool", m_gt, 1.0)
    aff(m_gt, ALU.is_gt, 0.0, 0, 1, [[-1, 128]])
    memset("pool", m_lt, 1.0)
    aff(m_lt, ALU.is_gt, 0.0, 0, -1, [[1, 128]])
    memset("pool", tri_le, 1.0)
    aff(tri_le, ALU.is_ge, 0.0, 0, -1, [[1, 128]])
    memset("pool", tri_ge, 1.0)
    aff(tri_ge, ALU.is_ge, 0.0, 0, 1, [[-1, 128]])
    NEG = -300.0
    memset("pool", neg_s_f, NEG)
    aff(neg_s_f, ALU.is_ge, 0.0, 0, 1, [[-1, 128]])
    memset("pool", neg_i_f, NEG)
    aff(neg_i_f, ALU.is_gt, 0.0, 0, 1, [[-1, 128]])
    memset("pool", neg_s_b, NEG)
    aff(neg_s_b, ALU.is_ge, 0.0, 0, -1, [[1, 128]])
    memset("pool", neg_i_b, NEG)
    aff(neg_i_b, ALU.is_gt, 0.0, 0, -1, [[1, 128]])
    memset("pool", blkmask, 1.0)
    aff(blkmask, ALU.is_ge, 0.0, 0, 1, [[-16, 8]])
    aff(blkmask, ALU.is_gt, 0.0, 16, -1, [[16, 8]])
    colmask_f = sb("colmask_f", [128, 8, 128], F32)
    memset("pool", colmask_f, 1.0)
    p.op("pool", lambda e: e.affine_select(out=colmask_f.ap, in_=colmask_f.ap, compare_op=ALU.is_ge, fill=0.0, base=0,
                                           pattern=[[-16, 8], [1, 128]], channel_multiplier=0),
         reads=[colmask_f.b], writes=[colmask_f.b])
    p.op("pool", lambda e: e.affine_select(out=colmask_f.ap, in_=colmask_f.ap, compare_op=ALU.is_gt, fill=0.0, base=16,
                                           pattern=[[16, 8], [-1, 128]], channel_multiplier=0),
         reads=[colmask_f.b], writes=[colmask_f.b])
    p.op("pool", lambda e: e.tensor_copy(out=colmask.ap, in_=colmask_f.ap), reads=[colmask_f.b], writes=[colmask.b])

    p.op("pool", lambda e: e.tensor_copy(out=II.ap[:, 0:128], in_=ident_f.ap), reads=[ident_f.b], writes=[II.b])
    p.op("pool", lambda e: e.tensor_copy(out=II.ap[:, 128:256], in_=ident_f.ap), reads=[ident_f.b], writes=[II.b])
    bd_prev = ident_f
    for lv in range(7):
        n2 = 2 << lv
        bdn = sb(f"bd{n2}", [128, 128], F32)
        memset("pool", bdn, 1.0)
        if n2 < 128:
            v3 = bdn.ap.rearrange("p (a b) -> p a b", b=n2)
            p.op("pool", lambda e, v3=v3, n2=n2: e.affine_select(out=v3, in_=v3, compare_op=ALU.is_ge, fill=0.0, base=0,
                                                                  pattern=[[-n2, 128 // n2], [0, n2]], channel_multiplier=1),
                 reads=[bdn.b], writes=[bdn.b])
            p.op("pool", lambda e, v3=v3, n2=n2: e.affine_select(out=v3, in_=v3, compare_op=ALU.is_gt, fill=0.0, base=n2,
                                                                  pattern=[[n2, 128 // n2], [0, n2]], channel_multiplier=-1),
                 reads=[bdn.b], writes=[bdn.b])
        dif = sb(f"dif{lv}", [128, 128], F32)
        p.op("pool", lambda e, dif=dif, bdn=bdn, bd_prev=bd_prev: e.tensor_tensor(out=dif.ap, in0=bdn.ap, in1=bd_prev.ap, op=ALU.subtract),
             reads=[bdn.b, bd_prev.b], writes=[dif.b])
        p.op("pool", lambda e, dif=dif, lv=lv: e.tensor_tensor(out=lvmask[0][lv].ap, in0=dif.ap, in1=m_lt.ap, op=ALU.mult),
             reads=[dif.b, m_lt.b], writes=[lvmask[0][lv].b])
        p.op("pool", lambda e, dif=dif, lv=lv: e.tensor_tensor(out=lvmask[1][lv].ap, in0=dif.ap, in1=m_gt.ap, op=ALU.mult),
             reads=[dif.b, m_gt.b], writes=[lvmask[1][lv].b])
        bd_prev = bdn
    def load_cols(dst, rows_ap, R, c0):
        stg = TB(ar.alloc([128], F32, "stg"))
        p.dma(stg.ap[0:R, :], rows_ap, writes=[stg.b])
        p.op("pe", lambda e: e.transpose(out=psb[0].ap[:, 0:R], in_=stg.ap[0:R, :], identity=ident_f.ap[0:R, 0:R]),
             reads=[stg.b, ident_f.b], writes=[psb[0].b])
        p.op("dve", lambda e: e.tensor_copy(out=dst.ap[:, c0:c0 + R], in_=psb[0].ap[:, 0:R]),
             reads=[psb[0].b], writes=[dst.b])

    ar.reset()
    load_cols(bg_sb, b_gate.rearrange("l (j q) -> (l j) q", q=128), 128, 0)
    load_cols(bglu_sb, b_glu.rearrange("l (j q) -> (l j) q", q=128), 32, 0)
    load_cols(dsk_sb, d_skip.rearrange("l (j q) -> (l j) q", q=128), 32, 0)
    load_cols(hng_sb, head_norm_g, 4, 0)
    cwv = conv_w.rearrange("l k (j q) -> (l k j) q", q=128)
    for i in range(4):
        load_cols(cw_sb, cwv[i * 120:(i + 1) * 120, :], 120, i * 120)
    p.barrier()

    PS = psb

    def subtb(tb, a, b_, name=""):
        return TB(tb.ap[:, a:b_], name)

    def phase_p2(l):
        ar.reset()
        xins = [TB(ar.alloc([T + 4], BF16, "xin")) for _ in range(2)]
        accs = [TB(ar.alloc([T], F32, "acc")) for _ in range(2)]
        sls = [TB(ar.alloc([T], F32, "sl")) for _ in range(2)]
        sqs = [TB(ar.alloc([T], F32, "sq")) for _ in range(2)]
        rinvs = [TB(ar.alloc([T], F32, "rinv")) for _ in range(2)]
        fmo = [TB(ar.alloc([T], BF16, "fmo")) for _ in range(2)]
        tmo = [TB(ar.alloc([16, 128], BF16, "tmo")) for _ in range(2)]
        for xin in xins:
            memset("pool", xin, 0.0)
        pi = 0
        import os as _os
        for ct in [int(v) for v in _os.environ.get('P2_CTS', ','.join(map(str, range(24)))).split(',')]:
            xin, acc, sl, sq, rinv, fo, to = [a[ct % 2] for a in (xins, accs, sls, sqs, rinvs, fmo, tmo)]
            p.dma(xin.ap[:, 2:T + 2], QKV.ap[ct * 128:(ct + 1) * 128, :], reads=[QKV.b], writes=[xin.b])
            cw = lambda k: cw_sb.ap[:, l * 120 + k * 24 + ct:l * 120 + k * 24 + ct + 1]
            p.op("dve", lambda e, acc=acc, xin=xin, c0=cw(0): e.tensor_scalar(out=acc.ap, in0=xin.ap[:, 0:T], scalar1=c0, scalar2=None,
                                                                              op0=ALU.mult), reads=[xin.b, cw_sb.b], writes=[acc.b])
            for k in range(1, 5):
                p.op("dve", lambda e, acc=acc, xin=xin, k=k, ck=cw(k): e.scalar_tensor_tensor(
                    out=acc.ap, in0=xin.ap[:, k:k + T], scalar=ck, in1=acc.ap, op0=ALU.mult, op1=ALU.add),
                    reads=[xin.b, cw_sb.b, acc.b], writes=[acc.b])
            if ct >= 16:
                p.op("act", lambda e, fo=fo, acc=acc: e.activation(out=fo.ap, in_=acc.ap, func=AF.Silu), reads=[acc.b], writes=[fo.b])
            else:
                p.op("act", lambda e, sl=sl, acc=acc: e.activation(out=sl.ap, in_=acc.ap, func=AF.Silu), reads=[acc.b], writes=[sl.b])
                p.op("act", lambda e, sl=sl, sq=sq: e.activation(out=sq.ap, in_=sl.ap, func=AF.Square), reads=[sl.b], writes=[sq.b])
                for tb in range(4):
                    ps = PS[2 + pi % 4]
                    pi += 1
                    p.op("pe", lambda e, ps=ps, sq=sq, tb=tb: e.matmul(ps.ap, lhsT=ones_f.ap, rhs=sq.ap[:, tb * 512:(tb + 1) * 512],
                                                                        start=True, stop=True), reads=[ones_f.b, sq.b], writes=[ps.b])
                    p.op("act", lambda e, ps=ps, rinv=rinv, tb=tb: e.activation(out=rinv.ap[:, tb * 512:(tb + 1) * 512], in_=ps.ap,
                                                                                 func=AF.Sqrt, bias=EPS), reads=[ps.b], writes=[rinv.b])
                p.op("dve", lambda e, rinv=rinv: e.reciprocal(out=rinv.ap, in_=rinv.ap), reads=[rinv.b], writes=[rinv.b])
                scale = (128.0 ** -0.5) if ct < 8 else 1.0
                p.op("dve", lambda e, fo=fo, sl=sl, rinv=rinv, scale=scale: e.scalar_tensor_tensor(
                    out=fo.ap, in0=sl.ap, scalar=scale, in1=rinv.ap, op0=ALU.mult, op1=ALU.mult), reads=[sl.b, rinv.b], writes=[fo.b])
                dstT = QN if ct < 8 else KN
                hh = ct % 8
                p.dma(dstT.ap[hh * 128:(hh + 1) * 128, :], fo.ap, reads=[fo.b], writes=[dstT.b])
            if ct >= 8:
                hh = ct % 8
                for half in range(2):
                    pt = PS[half]
                    ptv = pt.ap.bitcast(BF16).rearrange("p (c t) -> p c t", c=8)
                    for c in range(8):
                        blk = half * 8 + c
                        p.op("pe", lambda e, c=c, blk=blk, fo=fo, ptv=ptv: e.transpose(out=ptv[:, c, :], in_=fo.ap[:, blk * 128:(blk + 1) * 128],
                                                                                        identity=ident_b.ap),
                             reads=[fo.b, ident_b.b], writes=[pt.b])
                    dst = to.ap[:, half * 8:(half + 1) * 8, :]
                    if half == 0:
                        p.op("dve", lambda e, dst=dst, ptv=ptv: e.tensor_copy(out=dst, in_=ptv), reads=[pt.b], writes=[to.b])
                    else:
                        p.op("act", lambda e, dst=dst, ptv=ptv: e.activation(out=dst, in_=ptv, func=AF.Copy), reads=[pt.b], writes=[to.b])
                dT = KTM if ct < 16 else VTM
                p.dma(dT.ap[hh], to.ap, reads=[to.b], writes=[dT.b])

    def phase_p3(l, BAD):
        ar.reset()
        f3 = lambda nm: TB(ar.alloc([16, 16], F32, nm))
        BA = TB(ar.alloc([16, 32], F32, "BA"))
        BETA, LB, G, GC, EGC, BEG, GTt, EGT, KDS, SPt = [f3(n) for n in "BETA LB G GC EGC BEG GT EGT KDS SP".split()]
        alb = TB(ar.alloc([16], F32, "alb"))
        dtb = TB(ar.alloc([16], F32, "dtb"))
        nega = TB(ar.alloc([16], F32, "nega"))
        p.dma(BA.ap, BAD.ap, reads=[BAD.b], writes=[BA.b])
        p.dma(alb.ap, a_log[l].partition_broadcast(128), writes=[alb.b])
        p.dma(dtb.ap, dt_bias[l].partition_broadcast(128), writes=[dtb.b])
        p.op("act", lambda e: e.activation(out=BETA.ap, in_=BA.ap[:, :, 0:16], func=AF.Sigmoid), reads=[BA.b], writes=[BETA.b])
        p.op("act", lambda e: e.activation(out=LB.ap, in_=BETA.ap, func=AF.Ln), reads=[BETA.b], writes=[LB.b])
        p.op("act", lambda e: e.activation(out=nega.ap, in_=alb.ap, func=AF.Exp), reads=[alb.b], writes=[nega.b])
        p.op("dve", lambda e: e.tensor_scalar(out=nega.ap, in0=nega.ap, scalar1=-1.0, scalar2=None, op0=ALU.mult),
             reads=[nega.b], writes=[nega.b])
        p.op("dve", lambda e: e.tensor_tensor(out=SPt.ap, in0=BA.ap[:, :, 16:32], in1=dtb.ap.unsqueeze(1).to_broadcast([128, 16, 16]),
                                              op=ALU.add), reads=[BA.b, dtb.b], writes=[SPt.b])
        p.op("act", lambda e: e.activation(out=SPt.ap, in_=SPt.ap, func=AF.Exp), reads=[SPt.b], writes=[SPt.b])
        p.op("act", lambda e: e.activation(out=SPt.ap, in_=SPt.ap, func=AF.Ln, bias=1.0), reads=[SPt.b], writes=[SPt.b])
        p.op("dve", lambda e: e.tensor_tensor(out=G.ap, in0=SPt.ap, in1=nega.ap.unsqueeze(1).to_broadcast([128, 16, 16]), op=ALU.mult),
             reads=[SPt.b, nega.b], writes=[G.b])
        gcp = PS[6]
        gcv = gcp.ap[:, 0:256].rearrange("p (i n) -> p i n", i=16)
        gtv = gcp.ap[:, 256:512].rearrange("p (i n) -> p i n", i=16)
        for i in range(16):
            for d in range(2):
                tri = tri_le if d == 0 else tri_ge
                p.op("pe", lambda e, i=i, d=d, tri=tri: e.matmul(gcv[:, i, d * 8:(d + 1) * 8], lhsT=tri.ap, rhs=G.ap[:, i, d * 8:(d + 1) * 8],
                                                                  start=True, stop=True), reads=[tri.b, G.b], writes=[gcp.b])
            p.op("pe", lambda e, i=i: e.matmul(gtv[:, i, :], lhsT=ones_f.ap, rhs=G.ap[:, i, :], start=True, stop=True),
                 reads=[ones_f.b, G.b], writes=[gcp.b])
        p.op("dve", lambda e: e.tensor_copy(out=GC.ap, in_=gcv), reads=[gcp.b], writes=[GC.b])
        p.op("dve", lambda e: e.tensor_copy(out=GTt.ap, in_=gtv), reads=[gcp.b], writes=[GTt.b])
        p.op("act", lambda e: e.activation(out=EGC.ap, in_=GC.ap, func=AF.Exp), reads=[GC.b], writes=[EGC.b])
        p.op("act", lambda e: e.activation(out=EGT.ap, in_=GTt.ap, func=AF.Exp), reads=[GTt.b], writes=[EGT.b])
        p.op("dve", lambda e: e.tensor_tensor(out=KDS.ap, in0=GTt.ap, in1=GC.ap, op=ALU.subtract), reads=[GTt.b, GC.b], writes=[KDS.b])
        p.op("act", lambda e: e.activation(out=KDS.ap, in_=KDS.ap, func=AF.Exp), reads=[KDS.b], writes=[KDS.b])
        p.op("dve", lambda e: e.tensor_tensor(out=BEG.ap, in0=BETA.ap, in1=EGC.ap, op=ALU.mult), reads=[BETA.b, EGC.b], writes=[BEG.b])
        SC = [BETA.b, LB.b, G.b, GC.b, EGC.b, BEG.b, GTt.b, EGT.b, KDS.b]

        class HB:
            pass
        hbs = []
        sh_kT = TB(ar.alloc([T], BF16, "kT"))
        sh_kTM = TB(ar.alloc([16, 128], BF16, "kTM"))
        sh_vTM = TB(ar.alloc([16, 128], BF16, "vTM"))
        for par in range(2):
            hb = HB()
            hb.kT = sh_kT
            hb.qT = TB(ar.alloc([T], BF16, "qT"))
            hb.kTM = sh_kTM
            hb.vTM = sh_vTM
            hb.U = [[TB(ar.alloc([128], BF16, "U")) for i in range(16)] for d in range(2)]
            hb.wT = [[TB(ar.alloc([128], BF16, "wT")) for i in range(16)] for d in range(2)]
            hb.QKD = [[TB(ar.alloc([128], BF16, "QKD")) for i in range(16)] for d in range(2)]
            hb.kd = [[TB(ar.alloc([128], BF16, "kd")) for i in range(16)] for d in range(2)]
            hb.O = TB(ar.alloc([16, 128], F32, "O"))
            hb.Oi = [Buf("Oi") for i in range(16)]
            hb.Sf = [TB(ar.alloc([128], F32, "Sf")) for d in range(2)]
            hb.Sb = [TB(ar.alloc([128], BF16, "Sb")) for d in range(2)]
            hbs.append(hb)
        AQs = [TB(ar.alloc([256], F32, "AQ")) for _ in range(2)]
        tmpd = {}
        for d in range(2):
            tmpd[d] = dict(
                l1=[TB(ar.alloc([128], F32, "l1")) for _ in range(2)],
                r2=[TB(ar.alloc([128], F32, "r2")) for _ in range(2)],
                EE=[TB(ar.alloc([256], F32, "EE")) for _ in range(2)],
                LT=[TB(ar.alloc([128], BF16, "LT")) for _ in range(2)],
                Lm=[[TB(ar.alloc([128], BF16, "Lm")) for _ in range(2)] for _ in range(2)],
                Yn=[[TB(ar.alloc([128], BF16, "Yn")) for _ in range(2)] for _ in range(2)],
                TT=[[TB(ar.alloc([256], BF16, "TT")) for _ in range(2)] for _ in range(2)],
                X=[[TB(ar.alloc([256], BF16, "X")) for _ in range(1)] for _ in range(2)],
                wtm=[TB(ar.alloc([128], BF16, "wtm")) for _ in range(2)],
                vnew=[TB(ar.alloc([128], BF16, "vnew")) for _ in range(2)],
                tmp=[TB(ar.alloc([128], F32, "tmp")) for _ in range(2)],
                t2=[TB(ar.alloc([128], F32, "t2")) for _ in range(2)],
            )
        sqj = TB(ar.alloc([128], BF16, "sqj"))
        ssn = TB(ar.alloc([16], F32, "ssn"))
        On = TB(ar.alloc([16, 128], BF16, "On"))
        za1 = TB(ar.alloc([T], BF16, "za"))
        oaz1 = TB(ar.alloc([T], BF16, "oaz"))
        zas = [za1, za1]
        oazs = [oaz1, oaz1]
        AQp = subtb(PS[0], 0, 256, "AQp")
        TROp = TB(PS[0].ap[:, 256:512].bitcast(BF16).rearrange("p (c t) -> p c t", c=4), "TROp")
        EEp = [subtb(PS[1 + d], 0, 256, "EEp") for d in range(2)]
        YXp = [subtb(PS[1 + d], 256, 512, "YXp") for d in range(2)]
        b3 = PS[3].ap.bitcast(BF16)
        TRn = [TB(b3[:, d * 256:d * 256 + 128], "TRn") for d in range(2)]
        TRw = [TB(b3[:, d * 256 + 128:d * 256 + 256], "TRw") for d in range(2)]
        Tp = [subtb(PS[4 + d], 0, 256, "Tp") for d in range(2)]
        P1p = [subtb(PS[6 + d], 0, 128, "P1p") for d in range(2)]
        P23p = [subtb(PS[6 + d], 128, 384, "P23p") for d in range(2)]
        P4p = [subtb(PS[6 + d], 384, 512, "P4p") for d in range(2)]
        cnt = {"u": 0}

        def load_head(h):
            hb = hbs[h % 2]
            rs_ = slice(h * 128, (h + 1) * 128)
            p.dma(hb.kT.ap, KN.ap[rs_, :], reads=[KN.b], writes=[hb.kT.b])
            p.dma(hb.qT.ap, QN.ap[rs_, :], reads=[QN.b], writes=[hb.qT.b])
            p.dma(hb.kTM.ap, KTM.ap[h], reads=[KTM.b], writes=[hb.kTM.b])
            p.dma(hb.vTM.ap, VTM.ap[h], reads=[VTM.b], writes=[hb.vTM.b])
            for d in range(2):
                memset("pool", hb.Sf[d], 0.0)
                memset("pool", hb.Sb[d], 0.0)
            p.op("pool", lambda e, hb=hb: e.memset(hb.O.ap, 0.0), writes=[hb.O.b] + hb.Oi)

        def d1_unit(h, i):
            hb = hbs[h % 2]
            u = cnt["u"]
            cnt["u"] += 1
            par = u % 2
            blk = slice(i * 128, (i + 1) * 128)
            AQ = AQs[par]
            p.op("pe", lambda e: e.matmul(AQp.ap[:, 0:128], lhsT=hb.kT.ap[:, blk], rhs=hb.kT.ap[:, blk], start=True, stop=True),
                 reads=[hb.kT.b], writes=[AQp.b])
            p.op("pe", lambda e: e.matmul(AQp.ap[:, 128:256], lhsT=hb.kT.ap[:, blk], rhs=hb.qT.ap[:, blk], start=True, stop=True),
                 reads=[hb.kT.b, hb.qT.b], writes=[AQp.b])
            p.op("act", lambda e: e.activation(out=AQ.ap, in_=AQp.ap, func=AF.Copy), reads=[AQp.b], writes=[AQ.b])
            for d in range(2):
                col = d * 8 + h
                td = tmpd[d]
                l1, r2, EE, X, wtm = td["l1"][par], td["r2"][par], td["EE"][par], td["X"][par], td["wtm"][par]
                mk = m_gt if d == 0 else m_lt
                tri = tri_le if d == 0 else tri_ge
                ngs = neg_s_f if d == 0 else neg_s_b
                ngi = neg_i_f if d == 0 else neg_i_b
                ev = "act" if d == 0 else "dve"

                def evac(out_ap, in_ap, reads, writes, ev=ev):
                    if ev == "act":
                        p.op("act", lambda e: e.activation(out=out_ap, in_=in_ap, func=AF.Copy), reads=reads, writes=writes)
                    else:
                        p.op("dve", lambda e: e.tensor_copy(out=out_ap, in_=in_ap), reads=reads, writes=writes)
                gcol = G.ap[:, i, col:col + 1]
                lbcol = LB.ap[:, i, col:col + 1]
                p.op("pool", lambda e, l1=l1, mk=mk, gcol=gcol: e.tensor_scalar(out=l1.ap, in0=mk.ap, scalar1=gcol, scalar2=None, op0=ALU.mult),
                     reads=[mk.b, G.b], writes=[l1.b])
                p.op("pool", lambda e, r2=r2, lbcol=lbcol: e.tensor_scalar(out=r2.ap, in0=ident_f.ap, scalar1=lbcol, scalar2=None, op0=ALU.mult),
                     reads=[ident_f.b, LB.b], writes=[r2.b])
                ep = EEp[d]
                p.op("pe", lambda e, ep=ep, l1=l1, tri=tri: e.matmul(ep.ap[:, 0:128], lhsT=l1.ap, rhs=tri.ap, start=True, stop=False),
                     reads=[l1.b, tri.b], writes=[ep.b])
                p.op("pe", lambda e, ep=ep, r2=r2: e.matmul(ep.ap[:, 0:128], lhsT=ones_f.ap, rhs=r2.ap, start=False, stop=False),
                     reads=[r2.b, ones_f.b], writes=[ep.b])
                p.op("pe", lambda e, ep=ep, ngs=ngs: e.matmul(ep.ap[:, 0:128], lhsT=ident_f.ap, rhs=ngs.ap, start=False, stop=True),
                     reads=[ngs.b, ident_f.b], writes=[ep.b])
                p.op("pe", lambda e, ep=ep, l1=l1, tri=tri: e.matmul(ep.ap[:, 128:256], lhsT=l1.ap, rhs=tri.ap, start=True, stop=False),
                     reads=[l1.b, tri.b], writes=[ep.b])
                p.op("pe", lambda e, ep=ep, ngi=ngi: e.matmul(ep.ap[:, 128:256], lhsT=ident_f.ap, rhs=ngi.ap, start=False, stop=True),
                     reads=[ngi.b, ident_f.b], writes=[ep.b])
                p.op("act", lambda e, EE=EE, ep=ep: e.activation(out=EE.ap, in_=ep.ap, func=AF.Exp), reads=[ep.b], writes=[EE.b])
                LT = td["LT"][par]
                p.op("pool", lambda e, LT=LT, EE=EE: e.tensor_tensor(out=LT.ap, in0=AQ.ap[:, 0:128], in1=EE.ap[:, 0:128], op=ALU.mult),
                     reads=[AQ.b, EE.b], writes=[LT.b])
                qkd = hb.QKD[d][i]
                p.op("pool", lambda e, qkd=qkd, EE=EE: e.tensor_tensor(out=qkd.ap, in0=AQ.ap[:, 128:256], in1=EE.ap[:, 128:256], op=ALU.mult),
                     reads=[AQ.b, EE.b], writes=[qkd.b])
                X0 = X[0]
                bcol = BETA.ap[:, i, col:col + 1]
                begc = BEG.ap[:, i, col:col + 1]
                kdsc = KDS.ap[:, i, col:col + 1]
                p.op("pool", lambda e, X0=X0, bcol=bcol: e.tensor_scalar(out=X0.ap[:, 0:128], in0=hb.vTM.ap[:, i, :], scalar1=bcol, scalar2=None,
                                                                          op0=ALU.mult), reads=[hb.vTM.b, BETA.b], writes=[X0.b])
                p.op("pool", lambda e, X0=X0, begc=begc: e.tensor_scalar(out=X0.ap[:, 128:256], in0=hb.kTM.ap[:, i, :], scalar1=begc, scalar2=None,
                                                                          op0=ALU.mult), reads=[hb.kTM.b, BEG.b], writes=[X0.b])
                kd = hb.kd[d][i]
                p.op("pool", lambda e, kd=kd, kdsc=kdsc: e.tensor_scalar(out=kd.ap, in0=hb.kTM.ap[:, i, :], scalar1=kdsc, scalar2=None,
                                                                          op0=ALU.mult), reads=[hb.kTM.b, KDS.b], writes=[kd.b])
                tp_ = Tp[d]
                yx = YXp[d]
                p.op("pe", lambda e, tp_=tp_: e.matmul(tp_.ap, lhsT=ident_b.ap, rhs=II.ap, start=True, stop=False),
                     reads=[ident_b.b, II.b], writes=[tp_.b])
                TTc = II
                for lv in range(7):
                    Lm, Yn = td["Lm"][par][lv % 2], td["Yn"][par][lv % 2]
                    mk_ = lvmask[d][lv]
                    p.op("pool", lambda e, Lm=Lm, LT=LT, mk_=mk_: e.tensor_tensor(out=Lm.ap, in0=LT.ap, in1=mk_.ap, op=ALU.mult),
                         reads=[LT.b, mk_.b], writes=[Lm.b])
                    p.op("pe", lambda e, yx=yx, Lm=Lm, TTc=TTc: e.matmul(yx.ap[:, 0:128], lhsT=Lm.ap, rhs=TTc.ap[:, 0:128], start=True, stop=True),
                         reads=[Lm.b, TTc.b], writes=[yx.b])
                    if ev == "act":
                        p.op("act", lambda e, Yn=Yn, yx=yx: e.activation(out=Yn.ap, in_=yx.ap[:, 0:128], func=AF.Copy, scale=-1.0),
                             reads=[yx.b], writes=[Yn.b])
                    else:
                        p.op("dve", lambda e, Yn=Yn, yx=yx: e.tensor_scalar(out=Yn.ap, in0=yx.ap[:, 0:128], scalar1=-1.0, scalar2=None, op0=ALU.mult),
                             reads=[yx.b], writes=[Yn.b])
                    if lv < 6:
                        p.op("pe", lambda e, tp_=tp_, TTc=TTc, Yn=Yn: e.matmul(tp_.ap[:, 0:128], lhsT=TTc.ap[:, 128:256], rhs=Yn.ap, start=False, stop=False),
                             reads=[TTc.b, Yn.b], writes=[tp_.b])
                    p.op("pe", lambda e, tp_=tp_, TTc=TTc, Yn=Yn, lv=lv: e.matmul(tp_.ap[:, 128:256], lhsT=Yn.ap, rhs=TTc.ap[:, 128:256], start=False, stop=(lv == 6)),
                         reads=[TTc.b, Yn.b], writes=[tp_.b])
                    TTn = td["TT"][par][lv % 2]
                    evac(TTn.ap, tp_.ap, [tp_.b], [TTn.b])
                    TTc = TTn
                p.op("pe", lambda e, yx=yx, TTc=TTc, X0=X0: e.matmul(yx.ap, lhsT=TTc.ap[:, 128:256], rhs=X0.ap, start=True, stop=True),
                     reads=[TTc.b, X0.b], writes=[yx.b])
                evac(hb.U[d][i].ap, yx.ap[:, 0:128], [yx.b], [hb.U[d][i].b])
                evac(wtm.ap, yx.ap[:, 128:256], [yx.b], [wtm.b])
                trw = TRw[d]
                p.op("pe", lambda e, trw=trw, wtm=wtm: e.transpose(out=trw.ap, in_=wtm.ap, identity=ident_b.ap),
                     reads=[wtm.b, ident_b.b], writes=[trw.b])
                evac(hb.wT[d][i].ap, trw.ap, [trw.b], [hb.wT[d][i].b])

        def d2_step(h, d, i, stepn):
            hb = hbs[h % 2]
            col = d * 8 + h
            td = tmpd[d]
            par = stepn % 2
            vnew, tmp, t2 = td["vnew"][par], td["tmp"][par], td["t2"][par]
            blk = slice(i * 128, (i + 1) * 128)
            Sf, Sb = hb.Sf[d], hb.Sb[d]
            p1, p23, p4 = P1p[d], P23p[d], P4p[d]
            wT, U, QKD, kd = hb.wT[d][i], hb.U[d][i], hb.QKD[d][i], hb.kd[d][i]
            p.op("pe", lambda e: e.matmul(p1.ap, lhsT=wT.ap, rhs=Sb.ap, start=True, stop=True), reads=[wT.b, Sb.b], writes=[p1.b])
            p.op("dve", lambda e: e.tensor_tensor(out=vnew.ap, in0=U.ap, in1=p1.ap, op=ALU.subtract), reads=[U.b, p1.b], writes=[vnew.b])
            p.op("pe", lambda e: e.matmul(p23.ap[:, 0:128], lhsT=hb.qT.ap[:, blk], rhs=Sb.ap, start=True, stop=True),
                 reads=[hb.qT.b, Sb.b], writes=[p23.b])
            p.op("pe", lambda e: e.matmul(p23.ap[:, 128:256], lhsT=QKD.ap, rhs=vnew.ap, start=True, stop=True),
                 reads=[QKD.b, vnew.b], writes=[p23.b])
            p.op("pe", lambda e: e.matmul(p4.ap, lhsT=kd.ap, rhs=vnew.ap, start=True, stop=True), reads=[kd.b, vnew.b], writes=[p4.b])
            egc = EGC.ap[:, i, col:col + 1]
            egt = EGT.ap[:, i, col:col + 1]
            p.op("act", lambda e: e.activation(out=tmp.ap, in_=p23.ap[:, 0:128], func=AF.Copy, scale=egc), reads=[p23.b, EGC.b], writes=[tmp.b])
            p.op("dve", lambda e: e.tensor_tensor(out=t2.ap, in0=tmp.ap, in1=p23.ap[:, 128:256], op=ALU.add), reads=[tmp.b, p23.b], writes=[t2.b])
            p.op("pool", lambda e: e.tensor_tensor(out=hb.O.ap[:, i, :], in0=hb.O.ap[:, i, :], in1=t2.ap, op=ALU.add),
                 reads=[t2.b, hb.Oi[i]], writes=[hb.Oi[i]])
            p.op("dve", lambda e: e.scalar_tensor_tensor(out=Sf.ap, in0=Sf.ap, scalar=egt, in1=p4.ap, op0=ALU.mult, op1=ALU.add),
                 reads=[Sf.b, p4.b, EGT.b], writes=[Sf.b])
            p.op("act", lambda e: e.activation(out=Sb.ap, in_=Sf.ap, func=AF.Copy), reads=[Sf.b], writes=[Sb.b])

        def post_head(h):
            hb = hbs[h % 2]
            za, oaz = zas[h % 2], oazs[h % 2]
            p.dma(za.ap, ZA.ap[h * 128:(h + 1) * 128, :], reads=[ZA.b], writes=[za.b])
            for i in range(16):
                p.op("act", lambda e, i=i: e.activation(out=sqj.ap, in_=hb.O.ap[:, i, :], func=AF.Square, accum_out=ssn.ap[:, i:i + 1]),
                     reads=hb.Oi + [hb.O.b], writes=[sqj.b, ssn.b])
            p.op("act", lambda e: e.activation(out=ssn.ap, in_=ssn.ap, func=AF.Sqrt, scale=1.0 / 128, bias=EPS), reads=[ssn.b], writes=[ssn.b])
            p.op("dve", lambda e: e.reciprocal(out=ssn.ap, in_=ssn.ap), reads=[ssn.b], writes=[ssn.b])
            p.op("dve", lambda e: e.tensor_tensor(out=On.ap, in0=hb.O.ap, in1=ssn.ap.unsqueeze(2).to_broadcast([128, 16, 128]), op=ALU.mult),
                 reads=hb.Oi + [hb.O.b, ssn.b], writes=[On.b])
            if debug:
                p.dma(dbg["ORAW"].ap[h], hb.O.ap,
                      reads=hb.Oi + [hb.O.b], writes=[dbg["ORAW"].b])
            for grp in range(4):
                for t in range(4):
                    bk = grp * 4 + t
                    p.op("pe", lambda e, t=t, bk=bk: e.transpose(out=TROp.ap[:, t, :], in_=On.ap[:, bk, :], identity=ident_b.ap),
                         reads=[On.b, ident_b.b], writes=[TROp.b])
                hcol = hng_sb.ap[:, l:l + 1]
                p.op("dve", lambda e, grp=grp, hcol=hcol: e.scalar_tensor_tensor(
                    out=oaz.ap[:, grp * 512:(grp + 1) * 512], in0=TROp.ap.rearrange("p c t -> p (c t)"), scalar=hcol,
                    in1=za.ap[:, grp * 512:(grp + 1) * 512], op0=ALU.mult, op1=ALU.mult),
                    reads=[TROp.b, hng_sb.b, za.b], writes=[oaz.b])
            p.dma(OAZ.ap[h * 128:(h + 1) * 128, :], oaz.ap, reads=[oaz.b], writes=[OAZ.b])

        load_head(0)
        for i in range(16):
            d1_unit(0, i)
        for h in range(8):
            if h + 1 < 8:
                load_head(h + 1)
            for j in range(16):
                if h + 1 < 8:
                    d1_unit(h + 1, j)
                d2_step(h, 0, j, j)
                d2_step(h, 1, 15 - j, j)
            post_head(h)

    def phase_p4(l):
        ar.reset()
        f2 = lambda nm: TB(ar.alloc([128], F32, nm))
        LR, LI, DT, LDR, LDI, KR, KIp = [f2(n) for n in "LR LI DT LDR LDI KR KIp".split()]
        PA = TB(ar.alloc([128, 11], F32, "PA"))
        PB = TB(ar.alloc([128, 11], F32, "PB"))
        sgn = TB(ar.alloc([1], F32, "sgn"))
        memset("pool", sgn, 1.0)
        p.op("pool", lambda e: e.memset(sgn.ap[0:64, :], -1.0), reads=[sgn.b], writes=[sgn.b])
        mark = ar.off
        stg = TB(ar.alloc([128], F32, "stg"))
        for (src, dstT) in ((lam_re, LR), (lam_im, LI)):
            p.dma(stg.ap[:, 0:64], src[l], writes=[stg.b])
            p.dma(stg.ap[:, 64:128], src[l], writes=[stg.b])
            p.op("pe", lambda e: e.transpose(out=PS[0].ap[:, 0:128], in_=stg.ap, identity=ident_f.ap), reads=[stg.b, ident_f.b], writes=[PS[0].b])
            p.op("dve", lambda e, dstT=dstT: e.tensor_copy(out=dstT.ap, in_=PS[0].ap[:, 0:128]), reads=[PS[0].b], writes=[dstT.b])
        p.dma(DT.ap, log_dt[l].partition_broadcast(128), writes=[DT.b])
        p.op("act", lambda e: e.activation(out=DT.ap, in_=DT.ap, func=AF.Exp), reads=[DT.b], writes=[DT.b])
        p.op("dve", lambda e: e.tensor_tensor(out=LDR.ap, in0=LR.ap, in1=DT.ap, op=ALU.mult), reads=[LR.b, DT.b], writes=[LDR.b])
        p.op("dve", lambda e: e.tensor_tensor(out=LDI.ap, in0=LI.ap, in1=DT.ap, op=ALU.mult), reads=[LI.b, DT.b], writes=[LDI.b])
        NK = 12
        ANG = TB(ar.alloc([NK, 128], F32, "ANG"))
        MAG = TB(ar.alloc([NK, 128], F32, "MAG"))
        Yt = TB(ar.alloc([NK, 128], F32, "Yt"))
        NI = TB(ar.alloc([NK, 128], I32, "NI"))
        NF = TB(ar.alloc([NK, 128], F32, "NF"))
        SN = TB(ar.alloc([NK, 128], F32, "SN"))
        CS = TB(ar.alloc([NK, 128], F32, "CS"))
        for k in range(NK):
            sc = float(2 ** k) if k < 11 else 1.0
            p.op("dve", lambda e, k=k, sc=sc: e.tensor_scalar(out=ANG.ap[:, k, :], in0=LDI.ap, scalar1=sc, scalar2=None, op0=ALU.mult),
                 reads=[LDI.b], writes=[ANG.b])
            p.op("act", lambda e, k=k, sc=sc: e.activation(out=MAG.ap[:, k, :], in_=LDR.ap, func=AF.Exp, scale=sc), reads=[LDR.b], writes=[MAG.b])

        def sinlike(dst, shift):
            p.op("dve", lambda e: e.tensor_scalar(out=Yt.ap, in0=ANG.ap, scalar1=1.0 / TWO_PI, scalar2=16.5 + shift, op0=ALU.mult, op1=ALU.add),
                 reads=[ANG.b], writes=[Yt.b])
            p.op("dve", lambda e: e.tensor_copy(out=NI.ap, in_=Yt.ap), reads=[Yt.b], writes=[NI.b])
            p.op("dve", lambda e: e.tensor_copy(out=NF.ap, in_=NI.ap), reads=[NI.b], writes=[NF.b])
            p.op("dve", lambda e: e.tensor_tensor(out=Yt.ap, in0=Yt.ap, in1=NF.ap, op=ALU.subtract), reads=[Yt.b, NF.b], writes=[Yt.b])
            p.op("dve", lambda e: e.tensor_scalar(out=NF.ap, in0=Yt.ap, scalar1=0.0, scalar2=None, op0=ALU.is_lt), reads=[Yt.b], writes=[NF.b])
            p.op("dve", lambda e: e.tensor_tensor(out=Yt.ap, in0=Yt.ap, in1=NF.ap, op=ALU.add), reads=[Yt.b, NF.b], writes=[Yt.b])
            sc_ = TWO_PI * (1.0 - 1e-6)
            p.op("act", lambda e: e.activation(out=dst.ap, in_=Yt.ap, func=AF.Sin, scale=sc_, bias=-math.pi * (1.0 - 1e-6)),
                 reads=[Yt.b], writes=[dst.b])
        sinlike(SN, 0.0)
        sinlike(CS, 0.25)
        p.op("dve", lambda e: e.tensor_tensor(out=CS.ap, in0=CS.ap, in1=MAG.ap, op=ALU.mult), reads=[CS.b, MAG.b], writes=[CS.b])
        p.op("dve", lambda e: e.tensor_tensor(out=SN.ap, in0=SN.ap, in1=MAG.ap, op=ALU.mult), reads=[SN.b, MAG.b], writes=[SN.b])
        p.op("dve", lambda e: e.tensor_copy(out=PA.ap.rearrange("p g k -> p k g"), in_=CS.ap[:, 0:11, :]), reads=[CS.b], writes=[PA.b])
        p.op("dve", lambda e: e.tensor_scalar(out=PB.ap.rearrange("p g k -> p k g"), in0=SN.ap[:, 0:11, :], scalar1=sgn.ap, scalar2=-1.0,
                                              op0=ALU.mult, op1=ALU.mult), reads=[SN.b, sgn.b], writes=[PB.b])
        nr, den, tA, tB = [f2(n) for n in "nr den tA tB".split()]
        lbr = CS.ap[:, 11, :]
        lbi = SN.ap[:, 11, :]
        p.op("dve", lambda e: e.tensor_scalar(out=nr.ap, in0=lbr, scalar1=-1.0, scalar2=None, op0=ALU.add), reads=[CS.b], writes=[nr.b])
        p.op("dve", lambda e: e.tensor_tensor(out=den.ap, in0=LR.ap, in1=LR.ap, op=ALU.mult), reads=[LR.b], writes=[den.b])
        p.op("dve", lambda e: e.tensor_tensor(out=tA.ap, in0=LI.ap, in1=LI.ap, op=ALU.mult), reads=[LI.b], writes=[tA.b])
        p.op("dve", lambda e: e.tensor_tensor(out=den.ap, in0=den.ap, in1=tA.ap, op=ALU.add), reads=[den.b, tA.b], writes=[den.b])
        p.op("dve", lambda e: e.reciprocal(out=den.ap, in_=den.ap), reads=[den.b], writes=[den.b])
        p.op("dve", lambda e: e.tensor_tensor(out=tA.ap, in0=nr.ap, in1=LR.ap, op=ALU.mult), reads=[nr.b, LR.b], writes=[tA.b])
        p.op("dve", lambda e: e.tensor_tensor(out=tB.ap, in0=lbi, in1=LI.ap, op=ALU.mult), reads=[SN.b, LI.b], writes=[tB.b])
        p.op("dve", lambda e: e.tensor_tensor(out=tA.ap, in0=tA.ap, in1=tB.ap, op=ALU.add), reads=[tA.b, tB.b], writes=[tA.b])
        p.op("dve", lambda e: e.tensor_tensor(out=KR.ap, in0=tA.ap, in1=den.ap, op=ALU.mult), reads=[tA.b, den.b], writes=[KR.b])
        p.op("dve", lambda e: e.tensor_tensor(out=tA.ap, in0=lbi, in1=LR.ap, op=ALU.mult), reads=[SN.b, LR.b], writes=[tA.b])
        p.op("dve", lambda e: e.tensor_tensor(out=tB.ap, in0=nr.ap, in1=LI.ap, op=ALU.mult), reads=[nr.b, LI.b], writes=[tB.b])
        p.op("dve", lambda e: e.tensor_tensor(out=tA.ap, in0=tA.ap, in1=tB.ap, op=ALU.subtract), reads=[tA.b, tB.b], writes=[tA.b])
        p.op("dve", lambda e: e.tensor_tensor(out=tA.ap, in0=tA.ap, in1=den.ap, op=ALU.mult), reads=[tA.b, den.b], writes=[tA.b])
        p.op("dve", lambda e: e.tensor_scalar(out=KIp.ap, in0=tA.ap, scalar1=sgn.ap, scalar2=None, op0=ALU.mult), reads=[tA.b, sgn.b], writes=[KIp.b])
        p.barrier()
        ar.off = mark
        import os as _os
        P4S = _os.environ.get("P4_STOP", "")
        if P4S == "tables":
            return
        YS = TB(ar.alloc([8, T], BF16, "YS"))
        uTs = [TB(ar.alloc([T], BF16, "uT")) for _ in range(2)]
        Ss = [TB(ar.alloc([T], BF16, "S")) for _ in range(3)]
        Ams = [TB(ar.alloc([11, 128], BF16, "Am")) for _ in range(2)]
        t1s = [TB(ar.alloc([11, 128], BF16, "t1")) for _ in range(2)]
        t2s = [TB(ar.alloc([11, 128], BF16, "t2")) for _ in range(2)]
        BTs = [[TB(ar.alloc([8, 128], BF16, "BT")) for d in range(2)] for _ in range(2)]
        CTs = [[TB(ar.alloc([8, 128], BF16, "CT")) for d in range(2)] for _ in range(2)]
        X1s = [TB(ar.alloc([8, 16], F32, "X1")) for _ in range(2)]
        X2s = [TB(ar.alloc([8, 16], F32, "X2")) for _ in range(2)]
        Bbs = [TB(ar.alloc([8, 16], F32, "Bb")) for _ in range(2)]
        Cns = [TB(ar.alloc([128], F32, "Cn")) for _ in range(2)]
        dskm = [TB(ar.alloc([128], BF16, "dskm")) for _ in range(2)]
        gx2 = [TB(ar.alloc([512], F32, "gx2")) for _ in range(2)]
        gsg = [TB(ar.alloc([512], F32, "gsg")) for _ in range(2)]
        yraw = [TB(ar.alloc([512], F32, "yraw")) for _ in range(2)]
        Yp = [PS[i] for i in range(4)]
        BUp = [PS[4], PS[5]]
        SCp = [PS[6], PS[7]]
        cn = {"bu": 0, "sc": 0, "unit": 0, "g": 0, "prep": 0}
        C_GELU = 2.0 * math.sqrt(2.0 / math.pi)

        for gt in range(8):
            uT = uTs[gt % 2]
            p.dma(uT.ap, UU.ap[gt * 128:(gt + 1) * 128, :], reads=[UU.b], writes=[uT.b])
            dk = dskm[gt % 2]
            dcol = dsk_sb.ap[:, l * 8 + gt:l * 8 + gt + 1]
            p.op("pool", lambda e, dk=dk, dcol=dcol: e.tensor_scalar(out=dk.ap, in0=ident_f.ap, scalar1=dcol, scalar2=None, op0=ALU.mult),
                 reads=[ident_f.b, dsk_sb.b], writes=[dk.b])
            for d in range(2):
                pr = cn["prep"] % 2
                cn["prep"] += 1
                X1, X2, Bb, Cn = X1s[pr], X2s[pr], Bbs[pr], Cns[pr]
                BT, CT = BTs[gt % 2][d], CTs[gt % 2][d]
                brv = b_re[l, d, gt * 8:(gt + 1) * 8].rearrange("g q c -> q g c")
                biv = b_im[l, d, gt * 8:(gt + 1) * 8].rearrange("g q c -> q g c")
                p.dma(X1.ap[0:64], brv, writes=[X1.b])
                p.dma(X1.ap[64:128], biv, writes=[X1.b])
                p.dma(X2.ap[0:64], biv, writes=[X2.b])
                p.dma(X2.ap[64:128], brv, writes=[X2.b])
                c0 = d * 64 + gt * 8
                krb = KR.ap[:, c0:c0 + 8].unsqueeze(2).to_broadcast([128, 8, 16])
                kib = KIp.ap[:, c0:c0 + 8].unsqueeze(2).to_broadcast([128, 8, 16])
                p.op("pool", lambda e, X1=X1, krb=krb: e.tensor_tensor(out=X1.ap, in0=X1.ap, in1=krb, op=ALU.mult), reads=[X1.b, KR.b], writes=[X1.b])
                p.op("pool", lambda e, X2=X2, kib=kib: e.tensor_tensor(out=X2.ap, in0=X2.ap, in1=kib, op=ALU.mult), reads=[X2.b, KIp.b], writes=[X2.b])
                p.op("pool", lambda e, X1=X1, X2=X2, Bb=Bb: e.tensor_tensor(out=Bb.ap, in0=X1.ap, in1=X2.ap, op=ALU.add), reads=[X1.b, X2.b], writes=[Bb.b])
                tp = SCp[cn["sc"] % 2]
                cn["sc"] += 1
                p.op("pe", lambda e, tp=tp, Bb=Bb: e.transpose(out=tp.ap[:, 0:128], in_=Bb.ap.rearrange("p g c -> p (g c)"), identity=ident_f.ap),
                     reads=[Bb.b, ident_f.b], writes=[tp.b])
                p.op("dve", lambda e, tp=tp, BT=BT: e.tensor_tensor(out=BT.ap, in0=tp.ap[:, 0:128].unsqueeze(1).to_broadcast([128, 8, 128]),
                                                                     in1=blkmask.ap.unsqueeze(2).to_broadcast([128, 8, 128]), op=ALU.mult),
                     reads=[tp.b, blkmask.b], writes=[BT.b])
                crv = c_re[l, d, gt * 8:(gt + 1) * 8].rearrange("g c q -> (g c) q")
                civ = c_im[l, d, gt * 8:(gt + 1) * 8].rearrange("g c q -> (g c) q")
                p.dma(Cn.ap[:, 0:64], crv, writes=[Cn.b])
                p.dma(Cn.ap[:, 64:128], civ, writes=[Cn.b])
                p.op("pool", lambda e, Cn=Cn: e.tensor_scalar(out=Cn.ap[:, 64:128], in0=Cn.ap[:, 64:128], scalar1=-1.0, scalar2=None, op0=ALU.mult),
                     reads=[Cn.b], writes=[Cn.b])
                tp2 = SCp[cn["sc"] % 2]
                cn["sc"] += 1
                p.op("pe", lambda e, tp2=tp2, Cn=Cn: e.transpose(out=tp2.ap[:, 0:128], in_=Cn.ap, identity=ident_f.ap),
                     reads=[Cn.b, ident_f.b], writes=[tp2.b])
                p.op("dve", lambda e, tp2=tp2, CT=CT: e.tensor_tensor(out=CT.ap, in0=tp2.ap[:, 0:128].unsqueeze(1).to_broadcast([128, 8, 128]),
                                                                       in1=colmask.ap, op=ALU.mult), reads=[tp2.b, colmask.b], writes=[CT.b])
            if P4S == "prep":
                return
            for tb in range(4):
                p.op("pe", lambda e, tb=tb, dk=dk, uT=uT: e.matmul(Yp[tb].ap, lhsT=dk.ap, rhs=uT.ap[:, tb * 512:(tb + 1) * 512], start=True, stop=False),
                     reads=[dk.b, uT.b], writes=[Yp[tb].b])
            for d in range(2):
                BT, CT = BTs[gt % 2][d], CTs[gt % 2][d]
                for gl in range(8):
                    un = cn["unit"]
                    cn["unit"] += 1
                    dg = d * 64 + gt * 8 + gl
                    Am, t1, t2, S = Ams[un % 2], t1s[un % 2], t2s[un % 2], Ss[un % 3]
                    pbb = PB.ap[:, dg, :].unsqueeze(2).to_broadcast([128, 11, 128])
                    pab = PA.ap[:, dg, :].unsqueeze(2).to_broadcast([128, 11, 128])
                    p.op("pool", lambda e, t1=t1, pbb=pbb: e.tensor_tensor(out=t1.ap, in0=swap_f.ap.unsqueeze(1).to_broadcast([128, 11, 128]), in1=pbb,
                                                                            op=ALU.mult), reads=[swap_f.b, PB.b], writes=[t1.b])
                    p.op("pool", lambda e, t2=t2, pab=pab: e.tensor_tensor(out=t2.ap, in0=ident_f.ap.unsqueeze(1).to_broadcast([128, 11, 128]), in1=pab,
                                                                            op=ALU.mult), reads=[ident_f.b, PA.b], writes=[t2.b])
                    p.op("pool", lambda e, Am=Am, t1=t1, t2=t2: e.tensor_tensor(out=Am.ap, in0=t1.ap, in1=t2.ap, op=ALU.add),
                         reads=[t1.b, t2.b], writes=[Am.b])
                    for tb in range(4):
                        bp = BUp[cn["bu"] % 2]
                        cn["bu"] += 1
                        ts_ = slice(tb * 512, (tb + 1) * 512)
                        p.op("pe", lambda e, bp=bp, BT=BT, gl=gl, uT=uT, ts_=ts_: e.matmul(bp.ap, lhsT=BT.ap[:, gl, :], rhs=uT.ap[:, ts_], start=True, stop=True),
                             reads=[BT.b, uT.b], writes=[bp.b])
                        if tb % 2 == 0:
                            p.op("act", lambda e, bp=bp, S=S, ts_=ts_: e.activation(out=S.ap[:, ts_], in_=bp.ap, func=AF.Copy), reads=[bp.b], writes=[S.b])
                        else:
                            p.op("dve", lambda e, bp=bp, S=S, ts_=ts_: e.tensor_copy(out=S.ap[:, ts_], in_=bp.ap), reads=[bp.b], writes=[S.b])

                    if P4S == "bu":
                        return

                    def level(k, src, dst, n):
                        for c0_ in range(0, n, 512):
                            nn = min(512, n - c0_)
                            sp_ = SCp[cn["sc"] % 2]
                            cn["sc"] += 1
                            s_ap = src[:, c0_:c0_ + nn]
                            d_ap = dst[:, c0_:c0_ + nn]
                            a_ap = Am.ap[:, k, :]
                            p.op("pe", lambda e, sp_=sp_, a_ap=a_ap, s_ap=s_ap, nn=nn: e.matmul(sp_.ap[:, 0:nn], lhsT=a_ap, rhs=s_ap, start=True, stop=True),
                                 reads=[Am.b, S.b], writes=[sp_.b])
                            p.op("dve", lambda e, sp_=sp_, d_ap=d_ap, nn=nn: e.tensor_tensor(out=d_ap, in0=d_ap, in1=sp_.ap[:, 0:nn], op=ALU.add),
                                 reads=[sp_.b, S.b], writes=[S.b])
                    for k in range(11):
                        hs = 1 << k
                        s = 2 * hs
                        n = T // s
                        if d == 0:
                            level(k, S.ap[:, hs - 1::s], S.ap[:, s - 1::s], n)
                        else:
                            level(k, S.ap[:, hs::s], S.ap[:, 0::s], n)
                    for k in range(9, -1, -1):
                        hs = 1 << k
                        s = 2 * hs
                        n = T // s - 1
                        if d == 0:
                            level(k, S.ap[:, s - 1::s][:, 0:n], S.ap[:, s + hs - 1::s], n)
                        else:
                            level(k, S.ap[:, s::s], S.ap[:, hs::s][:, 0:n], n)
                    if P4S == "scan":
                        return
                    last = (d == 1 and gl == 7)
                    for tb in range(4):
                        ts_ = slice(tb * 512, (tb + 1) * 512)
                        p.op("pe", lambda e, tb=tb, CT=CT, gl=gl, S=S, ts_=ts_, last=last: e.matmul(Yp[tb].ap, lhsT=CT.ap[:, gl, :], rhs=S.ap[:, ts_],
                                                                                                 start=False, stop=last),
                             reads=[CT.b, S.b], writes=[Yp[tb].b])
                    if P4S == "y1":
                        return
                if P4S == "d0":
                    return
            if P4S == "tile1":
                return
            for tb in range(4):
                g_ = cn["g"] % 2
                cn["g"] += 1
                x2, sg, yr = gx2[g_], gsg[g_], yraw[g_]
                yp = Yp[tb]
                ts_ = slice(tb * 512, (tb + 1) * 512)
                p.op("act", lambda e, x2=x2, yp=yp: e.activation(out=x2.ap, in_=yp.ap, func=AF.Square), reads=[yp.b], writes=[x2.b])
                p.op("dve", lambda e, x2=x2: e.tensor_scalar(out=x2.ap, in0=x2.ap, scalar1=0.044715, scalar2=1.0, op0=ALU.mult, op1=ALU.add),
                     reads=[x2.b], writes=[x2.b])
                p.op("dve", lambda e, x2=x2, yp=yp: e.tensor_tensor(out=x2.ap, in0=x2.ap, in1=yp.ap, op=ALU.mult), reads=[x2.b, yp.b], writes=[x2.b])
                p.op("act", lambda e, x2=x2, sg=sg: e.activation(out=sg.ap, in_=x2.ap, func=AF.Sigmoid, scale=C_GELU), reads=[x2.b], writes=[sg.b])
                if debug and not _os.environ.get("NO_S5Y"):
                    p.op("act", lambda e, yr=yr, yp=yp: e.activation(out=yr.ap, in_=yp.ap, func=AF.Copy), reads=[yp.b], writes=[yr.b])
                    p.dma(dbg["S5Y"].ap[gt * 128:(gt + 1) * 128, ts_], yr.ap, reads=[yr.b], writes=[dbg["S5Y"].b])
                p.op("dve", lambda e, sg=sg, yp=yp, ts_=ts_, gt=gt: e.tensor_tensor(out=YS.ap[:, gt, ts_], in0=sg.ap, in1=yp.ap, op=ALU.mult),
                     reads=[sg.b, yp.b], writes=[YS.b])
            if P4S == "gelu1":
                return
        if debug:
            p.dma(dbg["YSD"].ap.rearrange("(c q) t -> q c t", q=128), YS.ap, reads=[YS.b], writes=[dbg["YSD"].b])
        if P4S == "gelu":
            return
        wg = TB(ar.alloc([8, 1024], BF16, "wg"))
        zbs = [TB(ar.alloc([T], BF16, "zb")) for _ in range(2)]
        ybo = [TB(ar.alloc([T], BF16, "ybo")) for _ in range(2)]
        p.dma(wg.ap, w_glu[l].rearrange("(c q) n -> q c n", q=128), writes=[wg.b], q="pool")
        pi = 0
        for jo in range(8):
            zb, yo = zbs[jo % 2], ybo[jo % 2]
            p.dma(zb.ap, ZB.ap[jo * 128:(jo + 1) * 128, :], reads=[ZB.b], writes=[zb.b])
            bcol = bglu_sb.ap[:, l * 8 + jo:l * 8 + jo + 1]
            for tb in range(4):
                ps = PS[4 + pi % 4]
                sg = gsg[pi % 2]
                pi += 1
                ts_ = slice(tb * 512, (tb + 1) * 512)
                for kc in range(8):
                    p.op("pe", lambda e, ps=ps, kc=kc, jo=jo, ts_=ts_: e.matmul(ps.ap, lhsT=wg.ap[:, kc, jo * 128:(jo + 1) * 128], rhs=YS.ap[:, kc, ts_],
                                                                               start=(kc == 0), stop=(kc == 7)), reads=[wg.b, YS.b], writes=[ps.b])
                p.op("act", lambda e, ps=ps, sg=sg, bcol=bcol: e.activation(out=sg.ap, in_=ps.ap, func=AF.Sigmoid, bias=bcol),
                     reads=[ps.b, bglu_sb.b], writes=[sg.b])
                p.op("dve", lambda e, sg=sg, jo=jo, ts_=ts_: e.tensor_tensor(out=sg.ap, in0=sg.ap, in1=YS.ap[:, jo, ts_], op=ALU.mult),
                     reads=[sg.b, YS.b], writes=[sg.b])
                p.op("pool", lambda e, sg=sg, zb=zb, yo=yo, ts_=ts_: e.tensor_tensor(out=yo.ap[:, ts_], in0=sg.ap, in1=zb.ap[:, ts_], op=ALU.mult),
                     reads=[sg.b, zb.b], writes=[yo.b])
            p.dma(YBZ.ap[jo * 128:(jo + 1) * 128, :], yo.ap, reads=[yo.b], writes=[YBZ.b])

    def phase_p5(l):
        ar.reset()
        OAZs = TB(ar.alloc([8, T], BF16, "OAZs"))
        YBZs = TB(ar.alloc([8, T], BF16, "YBZs"))
        p.dma(OAZs.ap, OAZ.ap.rearrange("(c q) t -> q c t", q=128), reads=[OAZ.b], writes=[OAZs.b])
        p.dma(YBZs.ap, YBZ.ap.rearrange("(c q) t -> q c t", q=128), reads=[YBZ.b], writes=[YBZs.b])
        wpa_b = [TB(ar.alloc([8, 512], BF16, "wpa")) for _ in range(2)]
        wpb_b = [TB(ar.alloc([8, 512], BF16, "wpb")) for _ in range(2)]
        gas = [TB(ar.alloc([T], BF16, "ga")) for _ in range(2)]
        gbs = [TB(ar.alloc([T], BF16, "gb")) for _ in range(2)]
        t1s = [TB(ar.alloc([512], F32, "t1")) for _ in range(2)]
        t2s = [TB(ar.alloc([512], F32, "t2")) for _ in range(2)]
        mos = [TB(ar.alloc([T], BF16, "mo")) for _ in range(2)]
        wpa_l = w_pa[l].rearrange("(c q) n -> q c n", q=128)
        wpb_l = w_pb[l].rearrange("(c q) n -> q c n", q=128)
        pi = 0
        for jb in range(4):
            wa, wb_ = wpa_b[jb % 2], wpb_b[jb % 2]
            p.dma(wa.ap, wpa_l[:, :, jb * 512:(jb + 1) * 512], writes=[wa.b], q="pool")
            p.dma(wb_.ap, wpb_l[:, :, jb * 512:(jb + 1) * 512], writes=[wb_.b], q="pool")
            for sb_ in range(4):
                j = jb * 4 + sb_
                ga, gb, mo = gas[j % 2], gbs[j % 2], mos[j % 2]
                p.dma(ga.ap, GT.ap[j * 128:(j + 1) * 128, :], reads=[GT.b], writes=[ga.b])
                p.dma(gb.ap, GT.ap[2048 + j * 128:2048 + (j + 1) * 128, :], reads=[GT.b], writes=[gb.b])
                for tb in range(4):
                    pa_, pb_ = PS[(2 * pi) % 8], PS[(2 * pi + 1) % 8]
                    t1, t2 = t1s[pi % 2], t2s[pi % 2]
                    pi += 1
                    ts_ = slice(tb * 512, (tb + 1) * 512)
                    for kc in range(8):
                        p.op("pe", lambda e, pa_=pa_, wa=wa, kc=kc, sb_=sb_, ts_=ts_: e.matmul(
                            pa_.ap, lhsT=wa.ap[:, kc, sb_ * 128:(sb_ + 1) * 128], rhs=OAZs.ap[:, kc, ts_], start=(kc == 0), stop=(kc == 7)),
                            reads=[wa.b, OAZs.b], writes=[pa_.b])
                    for kc in range(8):
                        p.op("pe", lambda e, pb_=pb_, wb_=wb_, kc=kc, sb_=sb_, ts_=ts_: e.matmul(
                            pb_.ap, lhsT=wb_.ap[:, kc, sb_ * 128:(sb_ + 1) * 128], rhs=YBZs.ap[:, kc, ts_], start=(kc == 0), stop=(kc == 7)),
                            reads=[wb_.b, YBZs.b], writes=[pb_.b])
                    p.op("dve", lambda e, t1=t1, pa_=pa_, ga=ga, ts_=ts_: e.tensor_tensor(out=t1.ap, in0=pa_.ap, in1=ga.ap[:, ts_], op=ALU.mult),
                         reads=[pa_.b, ga.b], writes=[t1.b])
                    p.op("dve", lambda e, t2=t2, pb_=pb_, gb=gb, ts_=ts_: e.tensor_tensor(out=t2.ap, in0=pb_.ap, in1=gb.ap[:, ts_], op=ALU.mult),
                         reads=[pb_.b, gb.b], writes=[t2.b])
                    p.op("pool", lambda e, t1=t1, t2=t2, mo=mo, ts_=ts_: e.tensor_tensor(out=mo.ap[:, ts_], in0=t1.ap, in1=t2.ap, op=ALU.add),
                         reads=[t1.b, t2.b], writes=[mo.b])
                p.dma(MRG.ap[j * 128:(j + 1) * 128, :], mo.ap, reads=[mo.b], writes=[MRG.b])
        p.barrier()
        ar.reset()
        MRGs = TB(ar.alloc([16, T], BF16, "MRGs"))
        wout = TB(ar.alloc([16, D], BF16, "wout"))
        xts = [TB(ar.alloc([D], F32, "xt")) for _ in range(2)]
        xos = [TB(ar.alloc([D], F32, "xo")) for _ in range(2)]
        mv = MRG.ap.rearrange("(c q) t -> q c t", q=128)
        for hf in range(2):
            p.dma(MRGs.ap[:, hf * 8:(hf + 1) * 8, :], mv[:, hf * 8:(hf + 1) * 8, :], reads=[MRG.b], writes=[MRGs.b])
        wo_l = w_out[l].rearrange("(c q) n -> q c n", q=128)
        for nb in range(4):
            p.dma(wout.ap[:, :, nb * 512:(nb + 1) * 512], wo_l[:, :, nb * 512:(nb + 1) * 512], writes=[wout.b], q="pool")
        xsrc = x_in if l == 0 else XRES.ap
        xsrc_b = [] if l == 0 else [XRES.b]
        pi = 0
        for r in range(NT):
            xt, xo = xts[r % 2], xos[r % 2]
            p.dma(xt.ap, xsrc[r * 128:(r + 1) * 128, :], reads=xsrc_b, writes=[xt.b])
            for nb in range(4):
                ps = PS[pi % 8]
                pi += 1
                ns = slice(nb * 512, (nb + 1) * 512)
                for kc in range(16):
                    p.op("pe", lambda e, ps=ps, kc=kc, r=r, ns=ns: e.matmul(
                        ps.ap, lhsT=MRGs.ap[:, kc, r * 128:(r + 1) * 128], rhs=wout.ap[:, kc, ns], start=(kc == 0), stop=(kc == 15)),
                        reads=[MRGs.b, wout.b], writes=[ps.b])
                p.op("dve", lambda e, ps=ps, xt=xt, xo=xo, ns=ns: e.tensor_tensor(out=xo.ap[:, ns], in0=ps.ap, in1=xt.ap[:, ns], op=ALU.add),
                     reads=[ps.b, xt.b], writes=[xo.b])
            p.dma(XRES.ap[r * 128:(r + 1) * 128, :], xo.ap, reads=[xo.b], writes=[XRES.b])

    def phase_final():
        ar.reset()
        fgb = TB(ar.alloc([D], F32, "fgb"))
        xts = [TB(ar.alloc([D], F32, "xt")) for _ in range(2)]
        yos = [TB(ar.alloc([D], F32, "yo")) for _ in range(2)]
        junk = TB(ar.alloc([D], BF16, "junk"))
        ssq = [TB(ar.alloc([1], F32, "ss")) for _ in range(2)]
        p.dma(fgb.ap, final_g.partition_broadcast(128), writes=[fgb.b])
        for r in range(NT):
            xt, yo, ss = xts[r % 2], yos[r % 2], ssq[r % 2]
            p.dma(xt.ap, XRES.ap[r * 128:(r + 1) * 128, :], reads=[XRES.b], writes=[xt.b])
            p.op("act", lambda e, xt=xt, ss=ss: e.activation(out=junk.ap, in_=xt.ap, func=AF.Square, accum_out=ss.ap),
                 reads=[xt.b], writes=[junk.b, ss.b])
            p.op("act", lambda e, ss=ss: e.activation(out=ss.ap, in_=ss.ap, func=AF.Sqrt, scale=1.0 / D, bias=EPS), reads=[ss.b], writes=[ss.b])
            p.op("dve", lambda e, ss=ss: e.reciprocal(out=ss.ap, in_=ss.ap), reads=[ss.b], writes=[ss.b])
            p.op("dve", lambda e, xt=xt, ss=ss, yo=yo: e.scalar_tensor_tensor(out=yo.ap, in0=xt.ap, scalar=ss.ap, in1=fgb.ap,
                                                                               op0=ALU.mult, op1=ALU.mult),
                 reads=[xt.b, ss.b, fgb.b], writes=[yo.b])
            p.dma(out_d[r * 128:(r + 1) * 128, :], yo.ap, reads=[yo.b], writes=[OUTB])

    for l in range(nlayers):
        ar.reset()
        hT = TB(ar.alloc([16, T], BF16, "hT"))
        lngb = TB(ar.alloc([D], F32, "lngb"))
        xts = [TB(ar.alloc([D], F32, "xt")) for _ in range(2)]
        junk = TB(ar.alloc([D], BF16, "junk"))
        hbs = [TB(ar.alloc([D], BF16, "hb")) for _ in range(2)]
        ssq = [TB(ar.alloc([1], F32, "ss")) for _ in range(2)]
        rsq = [TB(ar.alloc([1], F32, "rs")) for _ in range(2)]
        wbs = [TB(ar.alloc([16, 512], BF16, "wb")) for _ in range(2)]
        obs = [TB(ar.alloc([T], BF16, "ob")) for _ in range(2)]
        ba_sb = TB(ar.alloc([16, 32], F32, "ba"))
        xsrc = x_in if l == 0 else XRES.ap
        xsrc_b = [] if l == 0 else [XRES.b]
        p.dma(lngb.ap, ln_g[l].partition_broadcast(128), writes=[lngb.b])
        for r in range(NT):
            xt, hb, ss, rs = xts[r % 2], hbs[r % 2], ssq[r % 2], rsq[r % 2]
            p.dma(xt.ap, xsrc[r * 128:(r + 1) * 128, :], reads=xsrc_b, writes=[xt.b])
            p.op("act", lambda e, xt=xt, ss=ss: e.activation(out=junk.ap, in_=xt.ap, func=AF.Square, accum_out=ss.ap),
                 reads=[xt.b], writes=[junk.b, ss.b])
            p.op("act", lambda e, ss=ss, rs=rs: e.activation(out=rs.ap, in_=ss.ap, func=AF.Sqrt, scale=1.0 / D, bias=EPS),
                 reads=[ss.b], writes=[rs.b])
            p.op("dve", lambda e, rs=rs: e.reciprocal(out=rs.ap, in_=rs.ap), reads=[rs.b], writes=[rs.b])
            p.op("dve", lambda e, xt=xt, rs=rs, hb=hb: e.scalar_tensor_tensor(out=hb.ap, in0=xt.ap, scalar=rs.ap, in1=lngb.ap,
                                                                               op0=ALU.mult, op1=ALU.mult),
                 reads=[xt.b, rs.b, lngb.b], writes=[hb.b])
            for half in range(2):
                pt = PS[(2 * r + half) % 2]
                ptv = pt.ap.bitcast(BF16).rearrange("p (c t) -> p c t", c=8)
                for c in range(8):
                    cc = half * 8 + c
                    p.op("pe", lambda e, c=c, cc=cc, hb=hb, ptv=ptv: e.transpose(out=ptv[:, c, :], in_=hb.ap[:, cc * 128:(cc + 1) * 128],
                                                                                  identity=ident_b.ap),
                         reads=[hb.b, ident_b.b], writes=[pt.b])
                dst = hT.ap[:, half * 8:(half + 1) * 8, r * 128:(r + 1) * 128]
                if half == 0:
                    p.op("dve", lambda e, dst=dst, ptv=ptv: e.tensor_copy(out=dst, in_=ptv), reads=[pt.b], writes=[hT.b])
                else:
                    p.op("act", lambda e, dst=dst, ptv=ptv: e.activation(out=dst, in_=ptv, func=AF.Copy), reads=[pt.b], writes=[hT.b])

        blocks = []
        for j in range(6):
            blocks.append((QKV, j * 512, j * 512, AF.Copy, None))
        for j in range(2):
            blocks.append((ZA, j * 512, 3072 + j * 512, AF.Silu, None))
        for j in range(2):
            blocks.append((UU, j * 512, 4128 + j * 512, AF.Copy, None))
        for j in range(2):
            blocks.append((ZB, j * 512, 5152 + j * 512, AF.Silu, None))
        for j in range(8):
            blocks.append((GT, j * 512, 6176 + j * 512, AF.Sigmoid, l * 32 + j * 4))
        w_l = w_in[l].rearrange("(c q) n -> q c n", q=128)
        bi = 0
        pi = 0
        oi = 0
        for (dst, row0, col0, func, bcol) in blocks:
            wb = wbs[bi % 2]
            bi += 1
            p.dma(wb.ap, w_l[:, :, col0:col0 + 512], writes=[wb.b], q="pool")
            for sub in range(4):
                ob = obs[oi % 2]
                oi += 1
                for tb in range(4):
                    ps = PS[2 + pi % 4]
                    pi += 1
                    for c in range(16):
                        p.op("pe", lambda e, ps=ps, wb=wb, c=c, sub=sub, tb=tb: e.matmul(
                            ps.ap, lhsT=wb.ap[:, c, sub * 128:(sub + 1) * 128], rhs=hT.ap[:, c, tb * 512:(tb + 1) * 512],
                            start=(c == 0), stop=(c == 15)), reads=[wb.b, hT.b], writes=[ps.b])
                    if bcol is None:
                        p.op("act", lambda e, ob=ob, ps=ps, tb=tb, func=func: e.activation(
                            out=ob.ap[:, tb * 512:(tb + 1) * 512], in_=ps.ap, func=func), reads=[ps.b], writes=[ob.b])
                    else:
                        bias = bg_sb.ap[:, bcol + sub:bcol + sub + 1]
                        p.op("act", lambda e, ob=ob, ps=ps, tb=tb, func=func, bias=bias: e.activation(
                            out=ob.ap[:, tb * 512:(tb + 1) * 512], in_=ps.ap, func=func, bias=bias),
                            reads=[ps.b, bg_sb.b], writes=[ob.b])
                r0 = row0 + sub * 128
                p.dma(dst.ap[r0:r0 + 128, :], ob.ap, reads=[ob.b], writes=[dst.b])
        wba = TB(ar.alloc([16, 32], BF16, "wba"))
        p.dma(wba.ap, w_l[:, :, 4096:4128], writes=[wba.b], q="pool")
        psba = PS[6]
        pbv = psba.ap.rearrange("p (r n) -> p r n", r=16)
        for r in range(NT):
            for c in range(16):
                p.op("pe", lambda e, r=r, c=c: e.matmul(pbv[:, r, :], lhsT=hT.ap[:, c, r * 128:(r + 1) * 128], rhs=wba.ap[:, c, :],
                                                         start=(c == 0), stop=(c == 15)), reads=[hT.b, wba.b], writes=[psba.b])
        p.op("dve", lambda e: e.tensor_copy(out=ba_sb.ap, in_=pbv), reads=[psba.b], writes=[ba_sb.b])
        BAD = dbg.get("BA") or dscr(f"BA{l}", [128, 16, 32], F32)
        p.dma(BAD.ap, ba_sb.ap, reads=[ba_sb.b], writes=[BAD.b])
        p.barrier()
        if stop_after == ("P1", l):
            break
        phase_p2(l)
        p.barrier()
        if stop_after == ("P2", l):
            break
        phase_p3(l, BAD)
        p.barrier()
        if stop_after == ("P3", l):
            break
        phase_p4(l)
        p.barrier()
        if stop_after == ("P4", l):
            break
        phase_p5(l)
        p.new_epoch()
        if stop_after == ("P5", l):
            break
    else:
        phase_final()
    p.barrier()
    p.build()
    return nc


def make_in_map(inputs, b):
    f = lambda a: np.ascontiguousarray(a, dtype=np.float32)
    m = {"x": f(inputs["x"][b])}
    for k in ["ln_g", "w_in", "conv_w", "head_norm_g", "b_re", "b_im", "c_re", "c_im", "d_skip", "w_glu", "b_glu",
              "w_pa", "w_pb", "b_gate", "w_out", "final_g"]:
        m[k] = f(inputs[k])
    m["a_log"] = f(inputs["a_log"]).reshape(DEPTH, 16)
    m["dt_bias"] = f(inputs["dt_bias"]).reshape(DEPTH, 16)
    m["lam_re"] = f(inputs["lam_re"]).reshape(DEPTH, 128, 64)
    m["lam_im"] = f(inputs["lam_im"]).reshape(DEPTH, 128, 64)
    m["log_dt"] = f(inputs["log_dt"]).reshape(DEPTH, 128)
    return m


_NC_CACHE = {}


def kernel(**inputs):
    if "nc" not in _NC_CACHE:
        _NC_CACHE["nc"] = build_program()
    nc = _NC_CACHE["nc"]
    nb = inputs["x"].shape[0]
    maps = [make_in_map(inputs, b) for b in range(nb)]
    in_maps = [maps[c % nb] for c in range(NCORES)]
    res = run_bass_kernel_spmd(nc, in_maps, core_ids=list(range(NCORES)))
    out = np.stack([np.asarray(res.results[b]["out"], dtype=np.float32) for b in range(nb)], 0)
    return out
```

```python
import math
import numpy as np
import concourse.bass as bass
import concourse.mybir as mybir
from concourse.bass_utils import run_bass_kernel_spmd

F32 = mybir.dt.float32
BF16 = mybir.dt.bfloat16
I32 = mybir.dt.int32
AF = mybir.ActivationFunctionType
ALU = mybir.AluOpType
AX = mybir.AxisListType

T = 2048
D = 2048
NT = 16
DEPTH = 4
PROJ = 10272
EPS = 1e-6
NCORES = 8
TWO_PI = 2.0 * math.pi


class Buf:
    __slots__ = ("name", "lw", "rd")

    def __init__(self, name=""):
        self.name = name
        self.lw = None
        self.rd = []


class Prog:
    ENG = ("pe", "act", "dve", "pool", "sp")
    NDMA = {"sp": 8, "pool": 4}

    def __init__(self, nc, same_engine_sync=True):
        self.nc = nc
        self.ops = {e: [] for e in self.ENG}
        self.cnt = {e: 0 for e in self.ENG}
        self.known = {e: {} for e in self.ENG}
        self.sems = {}
        self.dma_n = {q: 0 for q in self.NDMA}
        self.dma_last = {}
        self.same = same_engine_sync
        self.epoch = 0
        for e in ("pe", "act", "dve", "pool"):
            self.sems[("c", e, 0)] = nc.alloc_semaphore("s_" + e)
        for q, n in self.NDMA.items():
            for i in range(n):
                self.sems[("d", q, i)] = nc.alloc_semaphore(f"d_{q}{i}")

    def _deps(self, reads, writes):
        deps = {}

        def add(d):
            if d is None:
                return
            k, v = d
            if deps.get(k, 0) < v:
                deps[k] = v
        for b in reads:
            add(b.lw)
        for b in writes:
            add(b.lw)
            for r in b.rd:
                add(r)
        return deps

    def _emit_waits(self, eng, deps):
        kn = self.known[eng]
        lst = []
        for k, v in deps.items():
            if k[0] == "c" and k[1] == eng and (eng == "pe" or not self.same):
                continue
            if kn.get(k, 0) >= v:
                continue
            kn[k] = v
            lst.append((self.sems[k], v))
        return lst

    def _record(self, done, reads, writes):
        for b in reads:
            b.rd.append(done)
        for b in writes:
            b.lw = done
            b.rd = []

    def op(self, eng, fn, reads=(), writes=()):
        deps = self._deps(reads, writes)
        waits = self._emit_waits(eng, deps)
        self.cnt[eng] += 1
        done = (("c", eng, self.epoch), self.cnt[eng])
        sem = self.sems[("c", eng, self.epoch)]

        def emit(e, fn=fn, waits=waits, sem=sem):
            for s, v in waits:
                e.wait_ge(s, v)
            fn(e).then_inc(sem, 1)
        self.ops[eng].append(emit)
        self._record(done, reads, writes)
        return done

    def dma(self, out, in_, reads=(), writes=(), q="sp", **kw):
        n = self.dma_n[q]
        self.dma_n[q] += 1
        K = self.NDMA[q]
        key = ("d", q, n % K)
        gen = n // K
        deps = self._deps(reads, writes)
        if gen > 0 and deps.get(key, 0) < 16 * gen:
            deps[key] = 16 * gen
        waits = self._emit_waits(q, deps)
        done = (key, 16 * (gen + 1))
        self.dma_last[key] = 16 * (gen + 1)
        sem = self.sems[key]

        def emit(e, waits=waits, sem=sem, out=out, in_=in_, kw=kw):
            for s, v in waits:
                e.wait_ge(s, v)
            e.dma_start(out=out, in_=in_, **kw).then_inc(sem, 16)
        self.ops[q].append(emit)
        self._record(done, reads, writes)
        return done

    def barrier(self):
        cur = {}
        for e in ("pe", "act", "dve", "pool"):
            if self.cnt[e] > 0:
                cur[("c", e, self.epoch)] = self.cnt[e]
        for k, v in self.dma_last.items():
            cur[k] = v
        for eng in self.ENG:
            waits = self._emit_waits(eng, dict(cur))

            def emit(e, waits=waits):
                for s, v in waits:
                    e.wait_ge(s, v)
            self.ops[eng].append(emit)

    def new_epoch(self):
        self.barrier()
        self.epoch += 1
        for e in ("pe", "act", "dve", "pool"):
            self.sems[("c", e, self.epoch)] = self.nc.alloc_semaphore(f"s_{e}{self.epoch}")
            self.cnt[e] = 0

    def build(self):
        nc = self.nc
        ops = self.ops
        with nc.Block() as block:
            @block.sync
            def _(e):
                for f in ops["sp"]:
                    f(e)

            @block.tensor
            def _(e):
                for f in ops["pe"]:
                    f(e)

            @block.scalar
            def _(e):
                for f in ops["act"]:
                    f(e)

            @block.vector
            def _(e):
                for f in ops["dve"]:
                    f(e)

            @block.gpsimd
            def _(e):
                for f in ops["pool"]:
                    f(e)


class Arena:
    def __init__(self, t, words):
        self.t = t
        self.words = words
        self.off = 0

    def reset(self):
        self.off = 0

    def alloc(self, shape, dtype, name=""):
        n = 1
        for s in shape:
            n *= s
        nbytes = n * (2 if dtype == BF16 else 4)
        w = (nbytes + 3) // 4
        w = (w + 7) // 8 * 8
        assert self.off + w <= self.words, f"arena overflow {name} {self.off + w} > {self.words}"
        ap = self.t[:, self.off:self.off + w]
        self.off += w
        if dtype != F32:
            ap = ap.bitcast(dtype)
        ap = ap[:, :n]
        if len(shape) == 2:
            ap = ap.rearrange("p (a b) -> p a b", a=shape[0], b=shape[1])
        elif len(shape) == 3:
            ap = ap.rearrange("p (a b c) -> p a b c", a=shape[0], b=shape[1], c=shape[2])
        return ap


class TB:
    __slots__ = ("ap", "b")

    def __init__(self, ap, name=""):
        self.ap = ap
        self.b = Buf(name)


AR_WORDS = 45000


def build_program(stop_after=None, debug=False, nlayers=DEPTH):
    nc = bass.Bass("TRN2", target_bir_lowering=False)
    p = Prog(nc)
    okind = "ExternalOutput" if debug else "Internal"

    def din(name, shape):
        return nc.dram_tensor(name, list(shape), F32, kind="ExternalInput").ap()

    x_in = din("x", [T, D])
    ln_g = din("ln_g", [DEPTH, D])
    w_in = din("w_in", [DEPTH, D, PROJ])
    conv_w = din("conv_w", [DEPTH, 5, 3072])
    a_log = din("a_log", [DEPTH, 16])
    dt_bias = din("dt_bias", [DEPTH, 16])
    head_norm_g = din("head_norm_g", [DEPTH, 128])
    lam_re = din("lam_re", [DEPTH, 128, 64])
    lam_im = din("lam_im", [DEPTH, 128, 64])
    log_dt = din("log_dt", [DEPTH, 128])
    b_re = din("b_re", [DEPTH, 2, 64, 64, 16])
    b_im = din("b_im", [DEPTH, 2, 64, 64, 16])
    c_re = din("c_re", [DEPTH, 2, 64, 16, 64])
    c_im = din("c_im", [DEPTH, 2, 64, 16, 64])
    d_skip = din("d_skip", [DEPTH, 1024])
    w_glu = din("w_glu", [DEPTH, 1024, 1024])
    b_glu = din("b_glu", [DEPTH, 1024])
    w_pa = din("w_pa", [DEPTH, 1024, D])
    w_pb = din("w_pb", [DEPTH, 1024, D])
    b_gate = din("b_gate", [DEPTH, 4096])
    w_out = din("w_out", [DEPTH, D, D])
    final_g = din("final_g", [D])
    out_d = nc.dram_tensor("out", [T, D], F32, kind="ExternalOutput").ap()

    def dscr(name, shape, dt):
        t = nc.dram_tensor(name, list(shape), dt, kind=okind).ap()
        return TB(t, name)

    XRES = dscr("XRES", [T, D], F32)
    QKV = dscr("QKV", [3072, T], BF16)
    ZA = dscr("ZA", [1024, T], BF16)
    UU = dscr("UU", [1024, T], BF16)
    ZB = dscr("ZB", [1024, T], BF16)
    GT = dscr("GT", [4096, T], BF16)
    QN = dscr("QN", [1024, T], BF16)
    KN = dscr("KN", [1024, T], BF16)
    KTM = dscr("KTM", [8, 128, 16, 128], BF16)
    VTM = dscr("VTM", [8, 128, 16, 128], BF16)
    OAZ = dscr("OAZ", [1024, T], BF16)
    YBZ = dscr("YBZ", [1024, T], BF16)
    MRG = dscr("MRG", [D, T], BF16)
    YSD = dscr("YSD", [1024, T], BF16)
    OUTB = Buf("out")
    dbg = {}
    if debug:
        dbg["BA"] = dscr("BA", [128, 16, 32], F32)
        dbg["ORAW"] = dscr("ORAW", [8, 128, 16, 128], F32)
        dbg["S5Y"] = dscr("S5Y", [1024, T], F32)

    def sb(name, shape, dt):
        return TB(nc.alloc_sbuf_tensor(name, list(shape), dt)[:], name)

    ident_f = sb("ident_f", [128, 128], F32)
    ident_b = sb("ident_b", [128, 128], BF16)
    ones_f = sb("ones_f", [128, 128], F32)
    swap_f = sb("swap_f", [128, 128], F32)
    m_gt = sb("m_gt", [128, 128], F32)
    m_lt = sb("m_lt", [128, 128], F32)
    tri_le = sb("tri_le", [128, 128], F32)
    tri_ge = sb("tri_ge", [128, 128], F32)
    neg_s_f = sb("neg_s_f", [128, 128], F32)
    neg_i_f = sb("neg_i_f", [128, 128], F32)
    neg_s_b = sb("neg_s_b", [128, 128], F32)
    neg_i_b = sb("neg_i_b", [128, 128], F32)
    blkmask = sb("blkmask", [128, 8], F32)
    colmask = sb("colmask", [128, 8, 128], BF16)
    bg_sb = sb("bg_sb", [128, 128], F32)
    bglu_sb = sb("bglu_sb", [128, 32], F32)
    dsk_sb = sb("dsk_sb", [128, 32], F32)
    hng_sb = sb("hng_sb", [128, 4], F32)
    cw_sb = sb("cw_sb", [128, 480], F32)
    II = sb("II", [128, 256], BF16)
    ones_b = sb("ones_b", [128, 128], BF16)
    tri_b = [sb("tri_b0", [128, 128], BF16), sb("tri_b1", [128, 128], BF16)]
    negs_b = [sb("negs_b0", [128, 128], BF16), sb("negs_b1", [128, 128], BF16)]
    negi_b = [sb("negi_b0", [128, 128], BF16), sb("negi_b1", [128, 128], BF16)]
    lvmask = [[sb(f"lvm{d}_{lv}", [128, 128], BF16) for lv in range(7)] for d in range(2)]
    arena_t = nc.alloc_sbuf_tensor("arena", [128, AR_WORDS], F32)
    ar = Arena(arena_t[:], AR_WORDS)
    psb = [TB(nc.alloc_psum_tensor(f"ps{i}", [128, 512], F32)[:], f"ps{i}") for i in range(8)]

    V = lambda e: e

    def memset(eng, tb, val):
        p.op(eng, lambda e: e.memset(tb.ap, val), writes=[tb.b])

    def aff(tb, cmp, fill, base, cm, pat):
        p.op("pool", lambda e: e.affine_select(out=tb.ap, in_=tb.ap, compare_op=cmp, fill=fill, base=base,
                                               pattern=pat, channel_multiplier=cm),
             reads=[tb.b], writes=[tb.b])

    memset("pool", ident_f, 1.0)
    aff(ident_f, ALU.is_equal, 0.0, 0, 1, [[-1, 128]])
    p.op("pool", lambda e: e.tensor_copy(out=ident_b.ap, in_=ident_f.ap), reads=[ident_f.b], writes=[ident_b.b])
    memset("pool", ones_f, 1.0)
    swap2 = sb("swap2", [128, 128], F32)
    memset("pool", swap_f, 1.0)
    aff(swap_f, ALU.is_equal, 0.0, 64, 1, [[-1, 128]])
    memset("pool", swap2, 1.0)
    aff(swap2, ALU.is_equal, 0.0, -64, 1, [[-1, 128]])
    p.op("pool", lambda e: e.tensor_tensor(out=swap_f.ap, in0=swap_f.ap, in1=swap2.ap, op=ALU.add),
         reads=[swap_f.b, swap2.b], writes=[swap_f.b])
    memset("pool", m_gt, 1.0)
    aff(m_gt, ALU.is_gt, 0.0, 0, 1, [[-1, 128]])
    memset("pool", m_lt, 1.0)
    aff(m_lt, ALU.is_gt, 0.0, 0, -1, [[1, 128]])
    memset("pool", tri_le, 1.0)
    aff(tri_le, ALU.is_ge, 0.0, 0, -1, [[1, 128]])
    memset("pool", tri_ge, 1.0)
    aff(tri_ge, ALU.is_ge, 0.0, 0, 1, [[-1, 128]])
    NEG = -256.0
    memset("pool", neg_s_f, NEG)
    aff(neg_s_f, ALU.is_ge, 0.0, 0, 1, [[-1, 128]])
    memset("pool", neg_i_f, NEG)
    aff(neg_i_f, ALU.is_gt, 0.0, 0, 1, [[-1, 128]])
    memset("pool", neg_s_b, NEG)
    aff(neg_s_b, ALU.is_ge, 0.0, 0, -1, [[1, 128]])
    memset("pool", neg_i_b, NEG)
    aff(neg_i_b, ALU.is_gt, 0.0, 0, -1, [[1, 128]])
    memset("pool", blkmask, 1.0)
    aff(blkmask, ALU.is_ge, 0.0, 0, 1, [[-16, 8]])
    aff(blkmask, ALU.is_gt, 0.0, 16, -1, [[16, 8]])
    colmask_f = sb("colmask_f", [128, 8, 128], F32)
    memset("pool", colmask_f, 1.0)
    p.op("pool", lambda e: e.affine_select(out=colmask_f.ap, in_=colmask_f.ap, compare_op=ALU.is_ge, fill=0.0, base=0,
                                           pattern=[[-16, 8], [1, 128]], channel_multiplier=0),
         reads=[colmask_f.b], writes=[colmask_f.b])
    p.op("pool", lambda e: e.affine_select(out=colmask_f.ap, in_=colmask_f.ap, compare_op=ALU.is_gt, fill=0.0, base=16,
                                           pattern=[[16, 8], [-1, 128]], channel_multiplier=0),
         reads=[colmask_f.b], writes=[colmask_f.b])
    p.op("pool", lambda e: e.tensor_copy(out=colmask.ap, in_=colmask_f.ap), reads=[colmask_f.b], writes=[colmask.b])

    for (s_, d_) in ((ones_f, ones_b), (tri_le, tri_b[0]), (tri_ge, tri_b[1]), (neg_s_f, negs_b[0]), (neg_s_b, negs_b[1]),
                     (neg_i_f, negi_b[0]), (neg_i_b, negi_b[1])):
        p.op("pool", lambda e, s_=s_, d_=d_: e.tensor_copy(out=d_.ap, in_=s_.ap), reads=[s_.b], writes=[d_.b])
    p.op("pool", lambda e: e.tensor_copy(out=II.ap[:, 0:128], in_=ident_f.ap), reads=[ident_f.b], writes=[II.b])
    p.op("pool", lambda e: e.tensor_copy(out=II.ap[:, 128:256], in_=ident_f.ap), reads=[ident_f.b], writes=[II.b])
    bd_prev = ident_f
    for lv in range(7):
        n2 = 2 << lv
        bdn = sb(f"bd{n2}", [128, 128], F32)
        memset("pool", bdn, 1.0)
        if n2 < 128:
            v3 = bdn.ap.rearrange("p (a b) -> p a b", b=n2)
            p.op("pool", lambda e, v3=v3, n2=n2: e.affine_select(out=v3, in_=v3, compare_op=ALU.is_ge, fill=0.0, base=0,
                                                                  pattern=[[-n2, 128 // n2], [0, n2]], channel_multiplier=1),
                 reads=[bdn.b], writes=[bdn.b])
            p.op("pool", lambda e, v3=v3, n2=n2: e.affine_select(out=v3, in_=v3, compare_op=ALU.is_gt, fill=0.0, base=n2,
                                                                  pattern=[[n2, 128 // n2], [0, n2]], channel_multiplier=-1),
                 reads=[bdn.b], writes=[bdn.b])
        dif = sb(f"dif{lv}", [128, 128], F32)
        p.op("pool", lambda e, dif=dif, bdn=bdn, bd_prev=bd_prev: e.tensor_tensor(out=dif.ap, in0=bdn.ap, in1=bd_prev.ap, op=ALU.subtract),
             reads=[bdn.b, bd_prev.b], writes=[dif.b])
        p.op("pool", lambda e, dif=dif, lv=lv: e.tensor_tensor(out=lvmask[0][lv].ap, in0=dif.ap, in1=m_lt.ap, op=ALU.mult),
             reads=[dif.b, m_lt.b], writes=[lvmask[0][lv].b])
        p.op("pool", lambda e, dif=dif, lv=lv: e.tensor_tensor(out=lvmask[1][lv].ap, in0=dif.ap, in1=m_gt.ap, op=ALU.mult),
             reads=[dif.b, m_gt.b], writes=[lvmask[1][lv].b])
        bd_prev = bdn
    def load_cols(dst, rows_ap, R, c0):
        stg = TB(ar.alloc([128], F32, "stg"))
        p.dma(stg.ap[0:R, :], rows_ap, writes=[stg.b])
        p.op("pe", lambda e: e.transpose(out=psb[0].ap[:, 0:R], in_=stg.ap[0:R, :], identity=ident_f.ap[0:R, 0:R]),
             reads=[stg.b, ident_f.b], writes=[psb[0].b])
        p.op("dve", lambda e: e.tensor_copy(out=dst.ap[:, c0:c0 + R], in_=psb[0].ap[:, 0:R]),
             reads=[psb[0].b], writes=[dst.b])

    ar.reset()
    load_cols(bg_sb, b_gate.rearrange("l (j q) -> (l j) q", q=128), 128, 0)
    load_cols(bglu_sb, b_glu.rearrange("l (j q) -> (l j) q", q=128), 32, 0)
    load_cols(dsk_sb, d_skip.rearrange("l (j q) -> (l j) q", q=128), 32, 0)
    load_cols(hng_sb, head_norm_g, 4, 0)
    cwv = conv_w.rearrange("l k (j q) -> (l k j) q", q=128)
    for i in range(4):
        load_cols(cw_sb, cwv[i * 120:(i + 1) * 120, :], 120, i * 120)
    p.barrier()

    PS = psb

    def subtb(tb, a, b_, name=""):
        return TB(tb.ap[:, a:b_], name)

    def phase_p2(l):
        ar.reset()
        xins = [TB(ar.alloc([T + 4], BF16, "xin")) for _ in range(2)]
        accs = [TB(ar.alloc([T], F32, "acc")) for _ in range(2)]
        sls = [TB(ar.alloc([T], F32, "sl")) for _ in range(2)]
        sqs = [TB(ar.alloc([T], F32, "sq")) for _ in range(2)]
        rinvs = [TB(ar.alloc([T], F32, "rinv")) for _ in range(2)]
        fmo = [TB(ar.alloc([T], BF16, "fmo")) for _ in range(2)]
        tmo = [TB(ar.alloc([16, 128], BF16, "tmo")) for _ in range(2)]
        for xin in xins:
            memset("pool", xin, 0.0)
        pi = 0
        import os as _os
        for ct in [int(v) for v in _os.environ.get('P2_CTS', ','.join(map(str, range(24)))).split(',')]:
            xin, acc, sl, sq, rinv, fo, to = [a[ct % 2] for a in (xins, accs, sls, sqs, rinvs, fmo, tmo)]
            p.dma(xin.ap[:, 2:T + 2], QKV.ap[ct * 128:(ct + 1) * 128, :], reads=[QKV.b], writes=[xin.b])
            cw = lambda k: cw_sb.ap[:, l * 120 + k * 24 + ct:l * 120 + k * 24 + ct + 1]
            p.op("dve", lambda e, acc=acc, xin=xin, c0=cw(0): e.tensor_scalar(out=acc.ap, in0=xin.ap[:, 0:T], scalar1=c0, scalar2=None,
                                                                              op0=ALU.mult), reads=[xin.b, cw_sb.b], writes=[acc.b])
            for k in range(1, 5):
                p.op("dve", lambda e, acc=acc, xin=xin, k=k, ck=cw(k): e.scalar_tensor_tensor(
                    out=acc.ap, in0=xin.ap[:, k:k + T], scalar=ck, in1=acc.ap, op0=ALU.mult, op1=ALU.add),
                    reads=[xin.b, cw_sb.b, acc.b], writes=[acc.b])
            if ct >= 16:
                p.op("act", lambda e, fo=fo, acc=acc: e.activation(out=fo.ap, in_=acc.ap, func=AF.Silu), reads=[acc.b], writes=[fo.b])
            else:
                p.op("act", lambda e, sl=sl, acc=acc: e.activation(out=sl.ap, in_=acc.ap, func=AF.Silu), reads=[acc.b], writes=[sl.b])
                p.op("act", lambda e, sl=sl, sq=sq: e.activation(out=sq.ap, in_=sl.ap, func=AF.Square), reads=[sl.b], writes=[sq.b])
                for tb in range(4):
                    ps = PS[2 + pi % 4]
                    pi += 1
                    p.op("pe", lambda e, ps=ps, sq=sq, tb=tb: e.matmul(ps.ap, lhsT=ones_f.ap, rhs=sq.ap[:, tb * 512:(tb + 1) * 512],
                                                                        start=True, stop=True), reads=[ones_f.b, sq.b], writes=[ps.b])
                    p.op("act", lambda e, ps=ps, rinv=rinv, tb=tb: e.activation(out=rinv.ap[:, tb * 512:(tb + 1) * 512], in_=ps.ap,
                                                                                 func=AF.Sqrt, bias=EPS), reads=[ps.b], writes=[rinv.b])
                p.op("dve", lambda e, rinv=rinv: e.reciprocal(out=rinv.ap, in_=rinv.ap), reads=[rinv.b], writes=[rinv.b])
                scale = (128.0 ** -0.5) if ct < 8 else 1.0
                p.op("dve", lambda e, fo=fo, sl=sl, rinv=rinv, scale=scale: e.scalar_tensor_tensor(
                    out=fo.ap, in0=sl.ap, scalar=scale, in1=rinv.ap, op0=ALU.mult, op1=ALU.mult), reads=[sl.b, rinv.b], writes=[fo.b])
                dstT = QN if ct < 8 else KN
                hh = ct % 8
                p.dma(dstT.ap[hh * 128:(hh + 1) * 128, :], fo.ap, reads=[fo.b], writes=[dstT.b])
            if ct >= 8:
                hh = ct % 8
                for half in range(2):
                    pt = PS[half]
                    ptv = pt.ap.bitcast(BF16).rearrange("p (c t) -> p c t", c=8)
                    for c in range(8):
                        blk = half * 8 + c
                        p.op("pe", lambda e, c=c, blk=blk, fo=fo, ptv=ptv: e.transpose(out=ptv[:, c, :], in_=fo.ap[:, blk * 128:(blk + 1) * 128],
                                                                                        identity=ident_b.ap),
                             reads=[fo.b, ident_b.b], writes=[pt.b])
                    dst = to.ap[:, half * 8:(half + 1) * 8, :]
                    if half == 0:
                        p.op("dve", lambda e, dst=dst, ptv=ptv: e.tensor_copy(out=dst, in_=ptv), reads=[pt.b], writes=[to.b])
                    else:
                        p.op("act", lambda e, dst=dst, ptv=ptv: e.activation(out=dst, in_=ptv, func=AF.Copy), reads=[pt.b], writes=[to.b])
                dT = KTM if ct < 16 else VTM
                p.dma(dT.ap[hh], to.ap, reads=[to.b], writes=[dT.b])

    def phase_p3(l, BAD):
        ar.reset()
        f3 = lambda nm: TB(ar.alloc([16, 16], F32, nm))
        BA = TB(ar.alloc([16, 32], F32, "BA"))
        BETA, LB, G, GC, EGC, BEG, GTt, EGT, KDS, SPt = [f3(n) for n in "BETA LB G GC EGC BEG GT EGT KDS SP".split()]
        alb = TB(ar.alloc([16], F32, "alb"))
        dtb = TB(ar.alloc([16], F32, "dtb"))
        nega = TB(ar.alloc([16], F32, "nega"))
        p.dma(BA.ap, BAD.ap, reads=[BAD.b], writes=[BA.b])
        p.dma(alb.ap, a_log[l].partition_broadcast(128), writes=[alb.b])
        p.dma(dtb.ap, dt_bias[l].partition_broadcast(128), writes=[dtb.b])
        p.op("act", lambda e: e.activation(out=BETA.ap, in_=BA.ap[:, :, 0:16], func=AF.Sigmoid), reads=[BA.b], writes=[BETA.b])
        p.op("act", lambda e: e.activation(out=LB.ap, in_=BETA.ap, func=AF.Ln), reads=[BETA.b], writes=[LB.b])
        p.op("act", lambda e: e.activation(out=nega.ap, in_=alb.ap, func=AF.Exp), reads=[alb.b], writes=[nega.b])
        p.op("dve", lambda e: e.tensor_scalar(out=nega.ap, in0=nega.ap, scalar1=-1.0, scalar2=None, op0=ALU.mult),
             reads=[nega.b], writes=[nega.b])
        p.op("dve", lambda e: e.tensor_tensor(out=SPt.ap, in0=BA.ap[:, :, 16:32], in1=dtb.ap.unsqueeze(1).to_broadcast([128, 16, 16]),
                                              op=ALU.add), reads=[BA.b, dtb.b], writes=[SPt.b])
        p.op("act", lambda e: e.activation(out=SPt.ap, in_=SPt.ap, func=AF.Exp), reads=[SPt.b], writes=[SPt.b])
        p.op("act", lambda e: e.activation(out=SPt.ap, in_=SPt.ap, func=AF.Ln, bias=1.0), reads=[SPt.b], writes=[SPt.b])
        p.op("dve", lambda e: e.tensor_tensor(out=G.ap, in0=SPt.ap, in1=nega.ap.unsqueeze(1).to_broadcast([128, 16, 16]), op=ALU.mult),
             reads=[SPt.b, nega.b], writes=[G.b])
        gcp = PS[6]
        gcv = gcp.ap[:, 0:256].rearrange("p (i n) -> p i n", i=16)
        gtv = gcp.ap[:, 256:512].rearrange("p (i n) -> p i n", i=16)
        for i in range(16):
            for d in range(2):
                tri = tri_le if d == 0 else tri_ge
                p.op("pe", lambda e, i=i, d=d, tri=tri: e.matmul(gcv[:, i, d * 8:(d + 1) * 8], lhsT=tri.ap, rhs=G.ap[:, i, d * 8:(d + 1) * 8],
                                                                  start=True, stop=True), reads=[tri.b, G.b], writes=[gcp.b])
            p.op("pe", lambda e, i=i: e.matmul(gtv[:, i, :], lhsT=ones_f.ap, rhs=G.ap[:, i, :], start=True, stop=True),
                 reads=[ones_f.b, G.b], writes=[gcp.b])
        p.op("dve", lambda e: e.tensor_copy(out=GC.ap, in_=gcv), reads=[gcp.b], writes=[GC.b])
        p.op("dve", lambda e: e.tensor_copy(out=GTt.ap, in_=gtv), reads=[gcp.b], writes=[GTt.b])
        p.op("act", lambda e: e.activation(out=EGC.ap, in_=GC.ap, func=AF.Exp), reads=[GC.b], writes=[EGC.b])
        p.op("act", lambda e: e.activation(out=EGT.ap, in_=GTt.ap, func=AF.Exp), reads=[GTt.b], writes=[EGT.b])
        p.op("dve", lambda e: e.tensor_tensor(out=KDS.ap, in0=GTt.ap, in1=GC.ap, op=ALU.subtract), reads=[GTt.b, GC.b], writes=[KDS.b])
        p.op("act", lambda e: e.activation(out=KDS.ap, in_=KDS.ap, func=AF.Exp), reads=[KDS.b], writes=[KDS.b])
        p.op("dve", lambda e: e.tensor_tensor(out=BEG.ap, in0=BETA.ap, in1=EGC.ap, op=ALU.mult), reads=[BETA.b, EGC.b], writes=[BEG.b])
        GHb = TB(ar.alloc([16, 16], BF16, "GHb"))
        LHb = TB(ar.alloc([16, 16], BF16, "LHb"))
        GH, GL, LH, LL = [f3(n) for n in "GH GL LH LL".split()]
        for (src_, hb_, hi_, lo_) in ((G, GHb, GH, GL), (LB, LHb, LH, LL)):
            p.op("dve", lambda e, src_=src_, hb_=hb_: e.tensor_copy(out=hb_.ap, in_=src_.ap), reads=[src_.b], writes=[hb_.b])
            p.op("dve", lambda e, hb_=hb_, hi_=hi_: e.tensor_copy(out=hi_.ap, in_=hb_.ap), reads=[hb_.b], writes=[hi_.b])
            p.op("dve", lambda e, src_=src_, hi_=hi_, lo_=lo_: e.tensor_tensor(out=lo_.ap, in0=src_.ap, in1=hi_.ap, op=ALU.subtract),
                 reads=[src_.b, hi_.b], writes=[lo_.b])
        SC = [BETA.b, LB.b, G.b, GC.b, EGC.b, BEG.b, GTt.b, EGT.b, KDS.b]

        class HB:
            pass
        hbs = []
        sh_kT = TB(ar.alloc([T], BF16, "kT"))
        sh_kTM = TB(ar.alloc([16, 128], BF16, "kTM"))
        sh_vTM = TB(ar.alloc([16, 128], BF16, "vTM"))
        for par in range(2):
            hb = HB()
            hb.kT = sh_kT
            hb.qT = TB(ar.alloc([T], BF16, "qT"))
            hb.kTM = sh_kTM
            hb.vTM = sh_vTM
            hb.U = [[TB(ar.alloc([128], BF16, "U")) for i in range(16)] for d in range(2)]
            hb.wT = [[TB(ar.alloc([128], BF16, "wT")) for i in range(16)] for d in range(2)]
            hb.QKD = [[TB(ar.alloc([128], BF16, "QKD")) for i in range(16)] for d in range(2)]
            hb.kd = [[TB(ar.alloc([128], BF16, "kd")) for i in range(16)] for d in range(2)]
            hb.O = TB(ar.alloc([16, 128], F32, "O"))
            hb.Oi = [Buf("Oi") for i in range(16)]
            hb.Sf = [TB(ar.alloc([128], F32, "Sf")) for d in range(2)]
            hb.Sb = [TB(ar.alloc([128], BF16, "Sb")) for d in range(2)]
            hbs.append(hb)
        AQs = [TB(ar.alloc([256], F32, "AQ")) for _ in range(2)]
        tmpd = {}
        for d in range(2):
            tmpd[d] = dict(
                l1=[TB(ar.alloc([256], BF16, "l1")) for _ in range(2)],
                r2=[TB(ar.alloc([256], BF16, "r2")) for _ in range(2)],
                EE=[TB(ar.alloc([256], F32, "EE")) for _ in range(2)],
                LT=[TB(ar.alloc([128], BF16, "LT")) for _ in range(2)],
                Lm=[[TB(ar.alloc([128], BF16, "Lm")) for _ in range(2)] for _ in range(2)],
                Yn=[[TB(ar.alloc([128], BF16, "Yn")) for _ in range(2)] for _ in range(2)],
                TT=[[TB(ar.alloc([256], BF16, "TT")) for _ in range(2)] for _ in range(2)],
                X=[[TB(ar.alloc([256], BF16, "X")) for _ in range(1)] for _ in range(2)],
                wtm=[TB(ar.alloc([128], BF16, "wtm")) for _ in range(2)],
                vnew=[TB(ar.alloc([128], BF16, "vnew")) for _ in range(2)],
                tmp=[TB(ar.alloc([128], F32, "tmp")) for _ in range(2)],
                t2=[TB(ar.alloc([128], F32, "t2")) for _ in range(2)],
            )
        sqj = TB(ar.alloc([128], BF16, "sqj"))
        ssn = TB(ar.alloc([16], F32, "ssn"))
        On = TB(ar.alloc([16, 128], BF16, "On"))
        za1 = TB(ar.alloc([T], BF16, "za"))
        oaz1 = TB(ar.alloc([T], BF16, "oaz"))
        zas = [za1, za1]
        oazs = [oaz1, oaz1]
        AQp = subtb(PS[0], 0, 256, "AQp")
        TROp = TB(PS[0].ap[:, 256:512].bitcast(BF16).rearrange("p (c t) -> p c t", c=4), "TROp")
        EEp = [subtb(PS[1 + d], 0, 256, "EEp") for d in range(2)]
        YXp = [subtb(PS[1 + d], 256, 512, "YXp") for d in range(2)]
        b3 = PS[3].ap.bitcast(BF16)
        TRn = [TB(b3[:, d * 256:d * 256 + 128], "TRn") for d in range(2)]
        TRw = [TB(b3[:, d * 256 + 128:d * 256 + 256], "TRw") for d in range(2)]
        Tp = [subtb(PS[4 + d], 0, 256, "Tp") for d in range(2)]
        P1p = [subtb(PS[6 + d], 0, 128, "P1p") for d in range(2)]
        P23p = [subtb(PS[6 + d], 128, 384, "P23p") for d in range(2)]
        P4p = [subtb(PS[6 + d], 384, 512, "P4p") for d in range(2)]
        cnt = {"u": 0}

        def load_head(h):
            hb = hbs[h % 2]
            rs_ = slice(h * 128, (h + 1) * 128)
            p.dma(hb.kT.ap, KN.ap[rs_, :], reads=[KN.b], writes=[hb.kT.b])
            p.dma(hb.qT.ap, QN.ap[rs_, :], reads=[QN.b], writes=[hb.qT.b])
            p.dma(hb.kTM.ap, KTM.ap[h], reads=[KTM.b], writes=[hb.kTM.b])
            p.dma(hb.vTM.ap, VTM.ap[h], reads=[VTM.b], writes=[hb.vTM.b])
            for d in range(2):
                memset("pool", hb.Sf[d], 0.0)
                memset("pool", hb.Sb[d], 0.0)
            p.op("pool", lambda e, hb=hb: e.memset(hb.O.ap, 0.0), writes=[hb.O.b] + hb.Oi)

        def d1_unit(h, i):
            hb = hbs[h % 2]
            u = cnt["u"]
            cnt["u"] += 1
            par = u % 2
            blk = slice(i * 128, (i + 1) * 128)
            AQ = AQs[par]
            p.op("pe", lambda e: e.matmul(AQp.ap[:, 0:128], lhsT=hb.kT.ap[:, blk], rhs=hb.kT.ap[:, blk], start=True, stop=True),
                 reads=[hb.kT.b], writes=[AQp.b])
            p.op("pe", lambda e: e.matmul(AQp.ap[:, 128:256], lhsT=hb.kT.ap[:, blk], rhs=hb.qT.ap[:, blk], start=True, stop=True),
                 reads=[hb.kT.b, hb.qT.b], writes=[AQp.b])
            p.op("act", lambda e: e.activation(out=AQ.ap, in_=AQp.ap, func=AF.Copy), reads=[AQp.b], writes=[AQ.b])
            import os as _os3
            _ylv = int(_os3.environ.get("LV_YIELD", "1"))

            def chain(d):
                col = d * 8 + h
                td = tmpd[d]
                l1, r2, EE, X, wtm = td["l1"][par], td["r2"][par], td["EE"][par], td["X"][par], td["wtm"][par]
                mk = m_gt if d == 0 else m_lt
                tri = tri_le if d == 0 else tri_ge
                ngs = neg_s_f if d == 0 else neg_s_b
                ngi = neg_i_f if d == 0 else neg_i_b
                ev = "act" if d == 0 else "dve"

                def evac(out_ap, in_ap, reads, writes, ev=ev):
                    if ev == "act":
                        p.op("act", lambda e: e.activation(out=out_ap, in_=in_ap, func=AF.Copy), reads=reads, writes=writes)
                    else:
                        p.op("dve", lambda e: e.tensor_copy(out=out_ap, in_=in_ap), reads=reads, writes=writes)
                ghc, glc = GH.ap[:, i, col:col + 1], GL.ap[:, i, col:col + 1]
                lhc, llc = LH.ap[:, i, col:col + 1], LL.ap[:, i, col:col + 1]
                for (dst_, msk_, sc_, dep_) in ((l1.ap[:, 0:128], mk, ghc, GH), (l1.ap[:, 128:256], mk, glc, GL)):
                    p.op("pool", lambda e, dst_=dst_, msk_=msk_, sc_=sc_: e.tensor_scalar(out=dst_, in0=msk_.ap, scalar1=sc_, scalar2=1.0,
                                                                                          op0=ALU.mult, op1=ALU.mult),
                         reads=[msk_.b, dep_.b], writes=[l1.b])
                for (dst_, sc_, dep_) in ((r2.ap[:, 0:128], lhc, LH), (r2.ap[:, 128:256], llc, LL)):
                    p.op("pool", lambda e, dst_=dst_, sc_=sc_: e.tensor_scalar(out=dst_, in0=ident_f.ap, scalar1=sc_, scalar2=1.0,
                                                                               op0=ALU.mult, op1=ALU.mult),
                         reads=[ident_f.b, dep_.b], writes=[r2.b])
                ep = EEp[d]
                trib = tri_b[d]
                seq = [(l1.ap[:, 0:128], trib.ap, [l1.b, trib.b]), (l1.ap[:, 128:256], trib.ap, [l1.b, trib.b]),
                       (ones_b.ap, r2.ap[:, 0:128], [r2.b, ones_b.b]), (ones_b.ap, r2.ap[:, 128:256], [r2.b, ones_b.b]),
                       (ident_b.ap, negs_b[d].ap, [negs_b[d].b, ident_b.b])]
                for n_, (lh_, rh_, rd_) in enumerate(seq):
                    p.op("pe", lambda e, ep=ep, lh_=lh_, rh_=rh_, n_=n_: e.matmul(ep.ap[:, 0:128], lhsT=lh_, rhs=rh_, start=(n_ == 0), stop=(n_ == 4)),
                         reads=rd_, writes=[ep.b])
                seq2 = [(l1.ap[:, 0:128], trib.ap, [l1.b, trib.b]), (l1.ap[:, 128:256], trib.ap, [l1.b, trib.b]),
                        (ident_b.ap, negi_b[d].ap, [negi_b[d].b, ident_b.b])]
                for n_, (lh_, rh_, rd_) in enumerate(seq2):
                    p.op("pe", lambda e, ep=ep, lh_=lh_, rh_=rh_, n_=n_: e.matmul(ep.ap[:, 128:256], lhsT=lh_, rhs=rh_, start=(n_ == 0), stop=(n_ == 2)),
                         reads=rd_, writes=[ep.b])
                p.op("act", lambda e, EE=EE, ep=ep: e.activation(out=EE.ap, in_=ep.ap, func=AF.Exp), reads=[ep.b], writes=[EE.b])
                yield
                LT = td["LT"][par]
                p.op("pool", lambda e, LT=LT, EE=EE: e.tensor_tensor(out=LT.ap, in0=AQ.ap[:, 0:128], in1=EE.ap[:, 0:128], op=ALU.mult),
                     reads=[AQ.b, EE.b], writes=[LT.b])
                qkd = hb.QKD[d][i]
                p.op("pool", lambda e, qkd=qkd, EE=EE: e.tensor_tensor(out=qkd.ap, in0=AQ.ap[:, 128:256], in1=EE.ap[:, 128:256], op=ALU.mult),
                     reads=[AQ.b, EE.b], writes=[qkd.b])
                X0 = X[0]
                bcol = BETA.ap[:, i, col:col + 1]
                begc = BEG.ap[:, i, col:col + 1]
                kdsc = KDS.ap[:, i, col:col + 1]
                p.op("act", lambda e, X0=X0, bcol=bcol: e.activation(out=X0.ap[:, 0:128], in_=hb.vTM.ap[:, i, :], func=AF.Copy, scale=bcol),
                     reads=[hb.vTM.b, BETA.b], writes=[X0.b])
                p.op("act", lambda e, X0=X0, begc=begc: e.activation(out=X0.ap[:, 128:256], in_=hb.kTM.ap[:, i, :], func=AF.Copy, scale=begc),
                     reads=[hb.kTM.b, BEG.b], writes=[X0.b])
                kd = hb.kd[d][i]
                p.op("pool", lambda e, kd=kd, kdsc=kdsc: e.tensor_scalar(out=kd.ap, in0=hb.kTM.ap[:, i, :], scalar1=kdsc, scalar2=1.0,
                                                                          op0=ALU.mult, op1=ALU.mult), reads=[hb.kTM.b, KDS.b], writes=[kd.b])
                tp_ = Tp[d]
                yx = YXp[d]
                p.op("pe", lambda e, tp_=tp_: e.matmul(tp_.ap, lhsT=ident_b.ap, rhs=II.ap, start=True, stop=True),
                     reads=[ident_b.b, II.b], writes=[tp_.b])
                yield
                TTc = II
                for lv in range(7):
                    Lm, Yn = td["Lm"][par][lv % 2], td["Yn"][par][lv % 2]
                    mk_ = lvmask[d][lv]
                    p.op("pool", lambda e, Lm=Lm, LT=LT, mk_=mk_: e.tensor_tensor(out=Lm.ap, in0=LT.ap, in1=mk_.ap, op=ALU.mult),
                         reads=[LT.b, mk_.b], writes=[Lm.b])
                    p.op("pe", lambda e, yx=yx, Lm=Lm, TTc=TTc: e.matmul(yx.ap[:, 0:128], lhsT=Lm.ap, rhs=TTc.ap[:, 0:128], start=True, stop=True),
                         reads=[Lm.b, TTc.b], writes=[yx.b])
                    if ev == "act":
                        p.op("act", lambda e, Yn=Yn, yx=yx: e.activation(out=Yn.ap, in_=yx.ap[:, 0:128], func=AF.Copy, scale=-1.0),
                             reads=[yx.b], writes=[Yn.b])
                    else:
                        p.op("dve", lambda e, Yn=Yn, yx=yx: e.tensor_scalar(out=Yn.ap, in0=yx.ap[:, 0:128], scalar1=-1.0, scalar2=None, op0=ALU.mult),
                             reads=[yx.b], writes=[Yn.b])
                    if _ylv == 2:
                        yield
                    if lv < 6:
                        p.op("pe", lambda e, tp_=tp_, TTc=TTc, Yn=Yn: e.matmul(tp_.ap[:, 0:128], lhsT=TTc.ap[:, 128:256], rhs=Yn.ap, start=False, stop=True, skip_group_check=True),
                             reads=[TTc.b, Yn.b], writes=[tp_.b])
                    p.op("pe", lambda e, tp_=tp_, TTc=TTc, Yn=Yn, lv=lv: e.matmul(tp_.ap[:, 128:256], lhsT=Yn.ap, rhs=TTc.ap[:, 128:256], start=False, stop=True, skip_group_check=True),
                         reads=[TTc.b, Yn.b], writes=[tp_.b])
                    TTn = td["TT"][par][lv % 2]
                    evac(TTn.ap, tp_.ap, [tp_.b], [TTn.b])
                    TTc = TTn
                    if _ylv:
                        yield
                p.op("pe", lambda e, yx=yx, TTc=TTc, X0=X0: e.matmul(yx.ap, lhsT=TTc.ap[:, 128:256], rhs=X0.ap, start=True, stop=True),
                     reads=[TTc.b, X0.b], writes=[yx.b])
                evac(hb.U[d][i].ap, yx.ap[:, 0:128], [yx.b], [hb.U[d][i].b])
                evac(wtm.ap, yx.ap[:, 128:256], [yx.b], [wtm.b])
                yield
                trw = TRw[d]
                p.op("pe", lambda e, trw=trw, wtm=wtm: e.transpose(out=trw.ap, in_=wtm.ap, identity=ident_b.ap),
                     reads=[wtm.b, ident_b.b], writes=[trw.b])
                evac(hb.wT[d][i].ap, trw.ap, [trw.b], [hb.wT[d][i].b])

            import os as _os2
            if _os2.environ.get("SEQ_CHAINS"):
                for _ in chain(0):
                    pass
                for _ in chain(1):
                    pass
                return
            gens = [chain(0), chain(1)]
            while gens:
                for g_ in list(gens):
                    try:
                        next(g_)
                    except StopIteration:
                        gens.remove(g_)

        def d2_step(h, d, i, stepn):
            hb = hbs[h % 2]
            col = d * 8 + h
            td = tmpd[d]
            par = stepn % 2
            vnew, tmp, t2 = td["vnew"][par], td["tmp"][par], td["t2"][par]
            blk = slice(i * 128, (i + 1) * 128)
            Sf, Sb = hb.Sf[d], hb.Sb[d]
            p1, p23, p4 = P1p[d], P23p[d], P4p[d]
            wT, U, QKD, kd = hb.wT[d][i], hb.U[d][i], hb.QKD[d][i], hb.kd[d][i]
            p.op("pe", lambda e: e.matmul(p1.ap, lhsT=wT.ap, rhs=Sb.ap, start=True, stop=True), reads=[wT.b, Sb.b], writes=[p1.b])
            p.op("dve", lambda e: e.tensor_tensor(out=vnew.ap, in0=U.ap, in1=p1.ap, op=ALU.subtract), reads=[U.b, p1.b], writes=[vnew.b])
            p.op("pe", lambda e: e.matmul(p23.ap[:, 0:128], lhsT=hb.qT.ap[:, blk], rhs=Sb.ap, start=True, stop=True),
                 reads=[hb.qT.b, Sb.b], writes=[p23.b])
            p.op("pe", lambda e: e.matmul(p23.ap[:, 128:256], lhsT=QKD.ap, rhs=vnew.ap, start=True, stop=True),
                 reads=[QKD.b, vnew.b], writes=[p23.b])
            p.op("pe", lambda e: e.matmul(p4.ap, lhsT=kd.ap, rhs=vnew.ap, start=True, stop=True), reads=[kd.b, vnew.b], writes=[p4.b])
            egc = EGC.ap[:, i, col:col + 1]
            egt = EGT.ap[:, i, col:col + 1]
            p.op("act", lambda e: e.activation(out=tmp.ap, in_=p23.ap[:, 0:128], func=AF.Copy, scale=egc), reads=[p23.b, EGC.b], writes=[tmp.b])
            p.op("dve", lambda e: e.tensor_tensor(out=t2.ap, in0=tmp.ap, in1=p23.ap[:, 128:256], op=ALU.add), reads=[tmp.b, p23.b], writes=[t2.b])
            p.op("pool", lambda e: e.tensor_tensor(out=hb.O.ap[:, i, :], in0=hb.O.ap[:, i, :], in1=t2.ap, op=ALU.add),
                 reads=[t2.b, hb.Oi[i]], writes=[hb.Oi[i]])
            p.op("dve", lambda e: e.scalar_tensor_tensor(out=Sf.ap, in0=Sf.ap, scalar=egt, in1=p4.ap, op0=ALU.mult, op1=ALU.add),
                 reads=[Sf.b, p4.b, EGT.b], writes=[Sf.b])
            p.op("act", lambda e: e.activation(out=Sb.ap, in_=Sf.ap, func=AF.Copy), reads=[Sf.b], writes=[Sb.b])

        def post_head(h):
            hb = hbs[h % 2]
            za, oaz = zas[h % 2], oazs[h % 2]
            p.dma(za.ap, ZA.ap[h * 128:(h + 1) * 128, :], reads=[ZA.b], writes=[za.b])
            for i in range(16):
                p.op("act", lambda e, i=i: e.activation(out=sqj.ap, in_=hb.O.ap[:, i, :], func=AF.Square, accum_out=ssn.ap[:, i:i + 1]),
                     reads=hb.Oi + [hb.O.b], writes=[sqj.b, ssn.b])
            p.op("act", lambda e: e.activation(out=ssn.ap, in_=ssn.ap, func=AF.Sqrt, scale=1.0 / 128, bias=EPS), reads=[ssn.b], writes=[ssn.b])
            p.op("dve", lambda e: e.reciprocal(out=ssn.ap, in_=ssn.ap), reads=[ssn.b], writes=[ssn.b])
            p.op("dve", lambda e: e.tensor_tensor(out=On.ap, in0=hb.O.ap, in1=ssn.ap.unsqueeze(2).to_broadcast([128, 16, 128]), op=ALU.mult),
                 reads=hb.Oi + [hb.O.b, ssn.b], writes=[On.b])
            if debug:
                p.dma(dbg["ORAW"].ap[h], hb.O.ap,
                      reads=hb.Oi + [hb.O.b], writes=[dbg["ORAW"].b])
            for grp in range(4):
                for t in range(4):
                    bk = grp * 4 + t
                    p.op("pe", lambda e, t=t, bk=bk: e.transpose(out=TROp.ap[:, t, :], in_=On.ap[:, bk, :], identity=ident_b.ap),
                         reads=[On.b, ident_b.b], writes=[TROp.b])
                hcol = hng_sb.ap[:, l:l + 1]
                p.op("dve", lambda e, grp=grp, hcol=hcol: e.scalar_tensor_tensor(
                    out=oaz.ap[:, grp * 512:(grp + 1) * 512], in0=TROp.ap.rearrange("p c t -> p (c t)"), scalar=hcol,
                    in1=za.ap[:, grp * 512:(grp + 1) * 512], op0=ALU.mult, op1=ALU.mult),
                    reads=[TROp.b, hng_sb.b, za.b], writes=[oaz.b])
            p.dma(OAZ.ap[h * 128:(h + 1) * 128, :], oaz.ap, reads=[oaz.b], writes=[OAZ.b])

        load_head(0)
        for i in range(16):
            d1_unit(0, i)
        for h in range(8):
            if h + 1 < 8:
                load_head(h + 1)
            for j in range(16):
                if h + 1 < 8:
                    d1_unit(h + 1, j)
                d2_step(h, 0, j, j)
                d2_step(h, 1, 15 - j, j)
            post_head(h)

    def phase_p4(l):
        ar.reset()
        f2 = lambda nm: TB(ar.alloc([128], F32, nm))
        LR, LI, DT, LDR, LDI, KR, KIp = [f2(n) for n in "LR LI DT LDR LDI KR KIp".split()]
        PA = TB(ar.alloc([128, 11], F32, "PA"))
        PB = TB(ar.alloc([128, 11], F32, "PB"))
        sgn = TB(ar.alloc([1], F32, "sgn"))
        memset("pool", sgn, 1.0)
        p.op("pool", lambda e: e.memset(sgn.ap[0:64, :], -1.0), reads=[sgn.b], writes=[sgn.b])
        mark = ar.off
        stg = TB(ar.alloc([128], F32, "stg"))
        for (src, dstT) in ((lam_re, LR), (lam_im, LI)):
            p.dma(stg.ap[:, 0:64], src[l], writes=[stg.b])
            p.dma(stg.ap[:, 64:128], src[l], writes=[stg.b])
            p.op("pe", lambda e: e.transpose(out=PS[0].ap[:, 0:128], in_=stg.ap, identity=ident_f.ap), reads=[stg.b, ident_f.b], writes=[PS[0].b])
            p.op("dve", lambda e, dstT=dstT: e.tensor_copy(out=dstT.ap, in_=PS[0].ap[:, 0:128]), reads=[PS[0].b], writes=[dstT.b])
        p.dma(DT.ap, log_dt[l].partition_broadcast(128), writes=[DT.b])
        p.op("act", lambda e: e.activation(out=DT.ap, in_=DT.ap, func=AF.Exp), reads=[DT.b], writes=[DT.b])
        p.op("dve", lambda e: e.tensor_tensor(out=LDR.ap, in0=LR.ap, in1=DT.ap, op=ALU.mult), reads=[LR.b, DT.b], writes=[LDR.b])
        p.op("dve", lambda e: e.tensor_tensor(out=LDI.ap, in0=LI.ap, in1=DT.ap, op=ALU.mult), reads=[LI.b, DT.b], writes=[LDI.b])
        NK = 12
        ANG = TB(ar.alloc([NK, 128], F32, "ANG"))
        MAG = TB(ar.alloc([NK, 128], F32, "MAG"))
        Yt = TB(ar.alloc([NK, 128], F32, "Yt"))
        NI = TB(ar.alloc([NK, 128], I32, "NI"))
        NF = TB(ar.alloc([NK, 128], F32, "NF"))
        SN = TB(ar.alloc([NK, 128], F32, "SN"))
        CS = TB(ar.alloc([NK, 128], F32, "CS"))
        for k in range(NK):
            sc = float(2 ** k) if k < 11 else 1.0
            p.op("dve", lambda e, k=k, sc=sc: e.tensor_scalar(out=ANG.ap[:, k, :], in0=LDI.ap, scalar1=sc, scalar2=None, op0=ALU.mult),
                 reads=[LDI.b], writes=[ANG.b])
            p.op("act", lambda e, k=k, sc=sc: e.activation(out=MAG.ap[:, k, :], in_=LDR.ap, func=AF.Exp, scale=sc), reads=[LDR.b], writes=[MAG.b])

        def sinlike(dst, shift):
            p.op("dve", lambda e: e.tensor_scalar(out=Yt.ap, in0=ANG.ap, scalar1=1.0 / TWO_PI, scalar2=16.5 + shift, op0=ALU.mult, op1=ALU.add),
                 reads=[ANG.b], writes=[Yt.b])
            p.op("dve", lambda e: e.tensor_copy(out=NI.ap, in_=Yt.ap), reads=[Yt.b], writes=[NI.b])
            p.op("dve", lambda e: e.tensor_copy(out=NF.ap, in_=NI.ap), reads=[NI.b], writes=[NF.b])
            p.op("dve", lambda e: e.tensor_tensor(out=Yt.ap, in0=Yt.ap, in1=NF.ap, op=ALU.subtract), reads=[Yt.b, NF.b], writes=[Yt.b])
            p.op("dve", lambda e: e.tensor_scalar(out=NF.ap, in0=Yt.ap, scalar1=0.0, scalar2=None, op0=ALU.is_lt), reads=[Yt.b], writes=[NF.b])
            p.op("dve", lambda e: e.tensor_tensor(out=Yt.ap, in0=Yt.ap, in1=NF.ap, op=ALU.add), reads=[Yt.b, NF.b], writes=[Yt.b])
            sc_ = TWO_PI * (1.0 - 1e-6)
            p.op("act", lambda e: e.activation(out=dst.ap, in_=Yt.ap, func=AF.Sin, scale=sc_, bias=-math.pi * (1.0 - 1e-6)),
                 reads=[Yt.b], writes=[dst.b])
        sinlike(SN, 0.0)
        sinlike(CS, 0.25)
        p.op("dve", lambda e: e.tensor_tensor(out=CS.ap, in0=CS.ap, in1=MAG.ap, op=ALU.mult), reads=[CS.b, MAG.b], writes=[CS.b])
        p.op("dve", lambda e: e.tensor_tensor(out=SN.ap, in0=SN.ap, in1=MAG.ap, op=ALU.mult), reads=[SN.b, MAG.b], writes=[SN.b])
        p.op("dve", lambda e: e.tensor_copy(out=PA.ap.rearrange("p g k -> p k g"), in_=CS.ap[:, 0:11, :]), reads=[CS.b], writes=[PA.b])
        p.op("dve", lambda e: e.tensor_scalar(out=PB.ap.rearrange("p g k -> p k g"), in0=SN.ap[:, 0:11, :], scalar1=sgn.ap, scalar2=-1.0,
                                              op0=ALU.mult, op1=ALU.mult), reads=[SN.b, sgn.b], writes=[PB.b])
        nr, den, tA, tB = [f2(n) for n in "nr den tA tB".split()]
        lbr = CS.ap[:, 11, :]
        lbi = SN.ap[:, 11, :]
        p.op("dve", lambda e: e.tensor_scalar(out=nr.ap, in0=lbr, scalar1=-1.0, scalar2=None, op0=ALU.add), reads=[CS.b], writes=[nr.b])
        p.op("dve", lambda e: e.tensor_tensor(out=den.ap, in0=LR.ap, in1=LR.ap, op=ALU.mult), reads=[LR.b], writes=[den.b])
        p.op("dve", lambda e: e.tensor_tensor(out=tA.ap, in0=LI.ap, in1=LI.ap, op=ALU.mult), reads=[LI.b], writes=[tA.b])
        p.op("dve", lambda e: e.tensor_tensor(out=den.ap, in0=den.ap, in1=tA.ap, op=ALU.add), reads=[den.b, tA.b], writes=[den.b])
        p.op("dve", lambda e: e.reciprocal(out=den.ap, in_=den.ap), reads=[den.b], writes=[den.b])
        p.op("dve", lambda e: e.tensor_tensor(out=tA.ap, in0=nr.ap, in1=LR.ap, op=ALU.mult), reads=[nr.b, LR.b], writes=[tA.b])
        p.op("dve", lambda e: e.tensor_tensor(out=tB.ap, in0=lbi, in1=LI.ap, op=ALU.mult), reads=[SN.b, LI.b], writes=[tB.b])
        p.op("dve", lambda e: e.tensor_tensor(out=tA.ap, in0=tA.ap, in1=tB.ap, op=ALU.add), reads=[tA.b, tB.b], writes=[tA.b])
        p.op("dve", lambda e: e.tensor_tensor(out=KR.ap, in0=tA.ap, in1=den.ap, op=ALU.mult), reads=[tA.b, den.b], writes=[KR.b])
        p.op("dve", lambda e: e.tensor_tensor(out=tA.ap, in0=lbi, in1=LR.ap, op=ALU.mult), reads=[SN.b, LR.b], writes=[tA.b])
        p.op("dve", lambda e: e.tensor_tensor(out=tB.ap, in0=nr.ap, in1=LI.ap, op=ALU.mult), reads=[nr.b, LI.b], writes=[tB.b])
        p.op("dve", lambda e: e.tensor_tensor(out=tA.ap, in0=tA.ap, in1=tB.ap, op=ALU.subtract), reads=[tA.b, tB.b], writes=[tA.b])
        p.op("dve", lambda e: e.tensor_tensor(out=tA.ap, in0=tA.ap, in1=den.ap, op=ALU.mult), reads=[tA.b, den.b], writes=[tA.b])
        p.op("dve", lambda e: e.tensor_scalar(out=KIp.ap, in0=tA.ap, scalar1=sgn.ap, scalar2=None, op0=ALU.mult), reads=[tA.b, sgn.b], writes=[KIp.b])
        p.barrier()
        ar.off = mark
        import os as _os
        P4S = _os.environ.get("P4_STOP", "")
        if P4S == "tables":
            return
        mark2 = ar.off
        uTs = [TB(ar.alloc([T], BF16, "uT")) for _ in range(2)]
        S8s = [TB(ar.alloc([8, T], BF16, "S8")) for _ in range(2)]
        Am8s = [TB(ar.alloc([88, 128], BF16, "Am8")) for _ in range(2)]
        t1h = TB(ar.alloc([44, 128], BF16, "t1h"))
        BTs = [[TB(ar.alloc([8, 128], BF16, "BT")) for d in range(2)] for _ in range(2)]
        CTs = [[TB(ar.alloc([8, 128], BF16, "CT")) for d in range(2)] for _ in range(2)]
        X1s = [TB(ar.alloc([8, 16], F32, "X1")) for _ in range(2)]
        X2s = [TB(ar.alloc([8, 16], F32, "X2")) for _ in range(2)]
        Bbs = [TB(ar.alloc([8, 16], F32, "Bb")) for _ in range(2)]
        Cns = [TB(ar.alloc([128], F32, "Cn")) for _ in range(2)]
        dskm = [TB(ar.alloc([128], BF16, "dskm")) for _ in range(2)]
        gx2 = [TB(ar.alloc([512], F32, "gx2"))] * 2
        gsg = [TB(ar.alloc([512], F32, "gsg"))] * 2
        yso = [TB(ar.alloc([T], BF16, "yso"))] * 2
        Yp = [PS[i] for i in range(4)]
        RB = [PS[4], PS[5], PS[6], PS[7]]
        cn = {"rb": 0, "g": 0, "prep": 0, "grp": 0}
        C_GELU = 2.0 * math.sqrt(2.0 / math.pi)

        def nextbank():
            b_ = RB[cn["rb"] % 4]
            cn["rb"] += 1
            return b_

        for gt in range(8):
            uT = uTs[gt % 2]
            p.dma(uT.ap, UU.ap[gt * 128:(gt + 1) * 128, :], reads=[UU.b], writes=[uT.b])
            dk = dskm[gt % 2]
            dcol = dsk_sb.ap[:, l * 8 + gt:l * 8 + gt + 1]
            p.op("pool", lambda e, dk=dk, dcol=dcol: e.tensor_scalar(out=dk.ap, in0=ident_f.ap, scalar1=dcol, scalar2=1.0, op0=ALU.mult, op1=ALU.mult),
                 reads=[ident_f.b, dsk_sb.b], writes=[dk.b])
            for d in range(2):
                pr = cn["prep"] % 2
                cn["prep"] += 1
                X1, X2, Bb, Cn = X1s[pr], X2s[pr], Bbs[pr], Cns[pr]
                BT, CT = BTs[gt % 2][d], CTs[gt % 2][d]
                brv = b_re[l, d, gt * 8:(gt + 1) * 8].rearrange("g q c -> q g c")
                biv = b_im[l, d, gt * 8:(gt + 1) * 8].rearrange("g q c -> q g c")
                p.dma(X1.ap[0:64], brv, writes=[X1.b])
                p.dma(X1.ap[64:128], biv, writes=[X1.b])
                p.dma(X2.ap[0:64], biv, writes=[X2.b])
                p.dma(X2.ap[64:128], brv, writes=[X2.b])
                c0 = d * 64 + gt * 8
                krb = KR.ap[:, c0:c0 + 8].unsqueeze(2).to_broadcast([128, 8, 16])
                kib = KIp.ap[:, c0:c0 + 8].unsqueeze(2).to_broadcast([128, 8, 16])
                p.op("pool", lambda e, X1=X1, krb=krb: e.tensor_tensor(out=X1.ap, in0=X1.ap, in1=krb, op=ALU.mult), reads=[X1.b, KR.b], writes=[X1.b])
                p.op("pool", lambda e, X2=X2, kib=kib: e.tensor_tensor(out=X2.ap, in0=X2.ap, in1=kib, op=ALU.mult), reads=[X2.b, KIp.b], writes=[X2.b])
                p.op("pool", lambda e, X1=X1, X2=X2, Bb=Bb: e.tensor_tensor(out=Bb.ap, in0=X1.ap, in1=X2.ap, op=ALU.add), reads=[X1.b, X2.b], writes=[Bb.b])
                tp = nextbank()
                p.op("pe", lambda e, tp=tp, Bb=Bb: e.transpose(out=tp.ap[:, 0:128], in_=Bb.ap.rearrange("p g c -> p (g c)"), identity=ident_f.ap),
                     reads=[Bb.b, ident_f.b], writes=[tp.b])
                p.op("act", lambda e, tp=tp, Bb=Bb: e.activation(out=Bb.ap.rearrange("p g c -> p (g c)"), in_=tp.ap[:, 0:128], func=AF.Copy),
                     reads=[tp.b], writes=[Bb.b])
                p.op("pool", lambda e, Bb=Bb, BT=BT: e.tensor_tensor(out=BT.ap, in0=Bb.ap.rearrange("p g c -> p (g c)").unsqueeze(1).to_broadcast([128, 8, 128]),
                                                                      in1=blkmask.ap.unsqueeze(2).to_broadcast([128, 8, 128]), op=ALU.mult),
                     reads=[Bb.b, blkmask.b], writes=[BT.b])
                crv = c_re[l, d, gt * 8:(gt + 1) * 8].rearrange("g c q -> (g c) q")
                civ = c_im[l, d, gt * 8:(gt + 1) * 8].rearrange("g c q -> (g c) q")
                p.dma(Cn.ap[:, 0:64], crv, writes=[Cn.b])
                p.dma(Cn.ap[:, 64:128], civ, writes=[Cn.b])
                p.op("pool", lambda e, Cn=Cn: e.tensor_scalar(out=Cn.ap[:, 64:128], in0=Cn.ap[:, 64:128], scalar1=-1.0, scalar2=1.0, op0=ALU.mult, op1=ALU.mult),
                     reads=[Cn.b], writes=[Cn.b])
                tp2 = nextbank()
                p.op("pe", lambda e, tp2=tp2, Cn=Cn: e.transpose(out=tp2.ap[:, 0:128], in_=Cn.ap, identity=ident_f.ap),
                     reads=[Cn.b, ident_f.b], writes=[tp2.b])
                p.op("act", lambda e, tp2=tp2, Cn=Cn: e.activation(out=Cn.ap, in_=tp2.ap[:, 0:128], func=AF.Copy), reads=[tp2.b], writes=[Cn.b])
                p.op("pool", lambda e, Cn=Cn, CT=CT: e.tensor_tensor(out=CT.ap, in0=Cn.ap.unsqueeze(1).to_broadcast([128, 8, 128]),
                                                                      in1=colmask.ap, op=ALU.mult), reads=[Cn.b, colmask.b], writes=[CT.b])
            for tb in range(4):
                p.op("pe", lambda e, tb=tb, dk=dk, uT=uT: e.matmul(Yp[tb].ap, lhsT=dk.ap, rhs=uT.ap[:, tb * 512:(tb + 1) * 512], start=True, stop=False),
                     reads=[dk.b, uT.b], writes=[Yp[tb].b])
            def group(d, gt=gt, uT=uT):
                BT, CT = BTs[gt % 2][d], CTs[gt % 2][d]
                Am8 = Am8s[d]
                S8 = S8s[d]
                dg0 = d * 64 + gt * 8
                A4 = Am8.ap.rearrange("p (u k) m -> p u k m", u=8)
                for hf in range(2):
                    us = slice(hf * 4, hf * 4 + 4)
                    pab = PA.ap[:, dg0 + hf * 4:dg0 + hf * 4 + 4, :].unsqueeze(3).to_broadcast([128, 4, 11, 128])
                    pbb = PB.ap[:, dg0 + hf * 4:dg0 + hf * 4 + 4, :].unsqueeze(3).to_broadcast([128, 4, 11, 128])
                    idb = ident_f.ap.unsqueeze(1).unsqueeze(1).to_broadcast([128, 4, 11, 128])
                    swb = swap_f.ap.unsqueeze(1).unsqueeze(1).to_broadcast([128, 4, 11, 128])
                    a4h = A4[:, us]
                    t4 = t1h.ap.rearrange("p (u k) m -> p u k m", u=4)
                    p.op("pool", lambda e, a4h=a4h, idb=idb, pab=pab: e.tensor_tensor(out=a4h, in0=idb, in1=pab, op=ALU.mult),
                         reads=[ident_f.b, PA.b], writes=[Am8.b])
                    p.op("pool", lambda e, t4=t4, swb=swb, pbb=pbb: e.tensor_tensor(out=t4, in0=swb, in1=pbb, op=ALU.mult),
                         reads=[swap_f.b, PB.b], writes=[t1h.b])
                    p.op("pool", lambda e, a4h=a4h, t4=t4: e.tensor_tensor(out=a4h, in0=a4h, in1=t4, op=ALU.add),
                         reads=[Am8.b, t1h.b], writes=[Am8.b])
                yield
                for gl in range(8):
                    for tb in range(4):
                        bp = nextbank()
                        ts_ = slice(tb * 512, (tb + 1) * 512)
                        p.op("pe", lambda e, bp=bp, BT=BT, gl=gl, uT=uT, ts_=ts_: e.matmul(bp.ap, lhsT=BT.ap[:, gl, :], rhs=uT.ap[:, ts_], start=True, stop=True),
                             reads=[BT.b, uT.b], writes=[bp.b])
                        if tb % 2 == 0:
                            p.op("act", lambda e, bp=bp, gl=gl, ts_=ts_: e.activation(out=S8.ap[:, gl, ts_], in_=bp.ap, func=AF.Copy), reads=[bp.b], writes=[S8.b])
                        else:
                            p.op("dve", lambda e, bp=bp, gl=gl, ts_=ts_: e.tensor_copy(out=S8.ap[:, gl, ts_], in_=bp.ap), reads=[bp.b], writes=[S8.b])

                yield

                def lvl(k, up, d=d, A4=A4, Am8=Am8, S8=S8):
                    hs = 1 << k
                    s = 2 * hs
                    if up:
                        n = T // s
                        src0, dst0 = (hs - 1, s - 1) if d == 0 else (hs, 0)
                    else:
                        n = T // s - 1
                        src0, dst0 = (s - 1, s + hs - 1) if d == 0 else (s, hs)
                    if n > 512:
                        for u in range(8):
                            for c0_ in range(0, n, 512):
                                nn = min(512, n - c0_)
                                bk = nextbank()
                                s_ap = S8.ap[:, u, src0 + c0_ * s::s][:, 0:nn]
                                d_ap = S8.ap[:, u, dst0 + c0_ * s::s][:, 0:nn]
                                a_ap = A4[:, u, k, :]
                                p.op("pe", lambda e, bk=bk, a_ap=a_ap, s_ap=s_ap, nn=nn: e.matmul(bk.ap[:, 0:nn], lhsT=a_ap, rhs=s_ap, start=True, stop=True),
                                     reads=[Am8.b, S8.b], writes=[bk.b])
                                p.op("dve", lambda e, bk=bk, d_ap=d_ap, nn=nn: e.tensor_tensor(out=d_ap, in0=d_ap, in1=bk.ap[:, 0:nn], op=ALU.add),
                                     reads=[bk.b, S8.b], writes=[S8.b])
                    else:
                        upb = min(8, 512 // n)
                        for u0 in range(0, 8, upb):
                            bk = nextbank()
                            for u in range(u0, u0 + upb):
                                s_ap = S8.ap[:, u, src0::s][:, 0:n]
                                a_ap = A4[:, u, k, :]
                                o_ap = bk.ap[:, (u - u0) * n:(u - u0 + 1) * n]
                                p.op("pe", lambda e, o_ap=o_ap, a_ap=a_ap, s_ap=s_ap: e.matmul(o_ap, lhsT=a_ap, rhs=s_ap, start=True, stop=True),
                                     reads=[Am8.b, S8.b], writes=[bk.b])
                            d_ap = S8.ap[:, u0:u0 + upb, dst0::s][:, :, 0:n]
                            i_ap = bk.ap[:, 0:upb * n].rearrange("p (u n) -> p u n", u=upb)
                            p.op("dve", lambda e, d_ap=d_ap, i_ap=i_ap: e.tensor_tensor(out=d_ap, in0=d_ap, in1=i_ap, op=ALU.add),
                                 reads=[bk.b, S8.b], writes=[S8.b])
                for k in range(11):
                    lvl(k, True)
                    yield
                for k in range(9, -1, -1):
                    lvl(k, False)
                    yield
                for gl in range(8):
                    last = (d == 1 and gl == 7)
                    for tb in range(4):
                        ts_ = slice(tb * 512, (tb + 1) * 512)
                        p.op("pe", lambda e, tb=tb, CT=CT, gl=gl, ts_=ts_, last=last: e.matmul(Yp[tb].ap, lhsT=CT.ap[:, gl, :], rhs=S8.ap[:, gl, ts_],
                                                                                            start=False, stop=last),
                             reads=[CT.b, S8.b], writes=[Yp[tb].b])
            gens = [group(0), group(1)]
            while gens:
                for g_ in list(gens):
                    try:
                        next(g_)
                    except StopIteration:
                        gens.remove(g_)
            yo_ = yso[gt % 2]
            for tb in range(4):
                g_ = cn["g"] % 2
                cn["g"] += 1
                x2, sg = gx2[g_], gsg[g_]
                yp = Yp[tb]
                ts_ = slice(tb * 512, (tb + 1) * 512)
                p.op("act", lambda e, x2=x2, yp=yp: e.activation(out=x2.ap, in_=yp.ap, func=AF.Square), reads=[yp.b], writes=[x2.b])
                p.op("pool", lambda e, x2=x2: e.tensor_scalar(out=x2.ap, in0=x2.ap, scalar1=0.044715, scalar2=1.0, op0=ALU.mult, op1=ALU.add),
                     reads=[x2.b], writes=[x2.b])
                p.op("dve", lambda e, x2=x2, yp=yp: e.tensor_tensor(out=x2.ap, in0=x2.ap, in1=yp.ap, op=ALU.mult), reads=[x2.b, yp.b], writes=[x2.b])
                p.op("act", lambda e, x2=x2, sg=sg: e.activation(out=sg.ap, in_=x2.ap, func=AF.Sigmoid, scale=C_GELU), reads=[x2.b], writes=[sg.b])
                p.op("dve", lambda e, sg=sg, yp=yp, ts_=ts_, yo_=yo_: e.tensor_tensor(out=yo_.ap[:, ts_], in0=sg.ap, in1=yp.ap, op=ALU.mult),
                     reads=[sg.b, yp.b], writes=[yo_.b])
            p.dma(YSD.ap[gt * 128:(gt + 1) * 128, :], yo_.ap, reads=[yo_.b], writes=[YSD.b])
        p.barrier()
        ar.off = mark2
        YS = TB(ar.alloc([8, T], BF16, "YS"))
        p.dma(YS.ap, YSD.ap.rearrange("(c q) t -> q c t", q=128), reads=[YSD.b], writes=[YS.b])
        gsg = [TB(ar.alloc([512], F32, "gsg2")) for _ in range(2)]
        wg = TB(ar.alloc([8, 1024], BF16, "wg"))
        zbs = [TB(ar.alloc([T], BF16, "zb")) for _ in range(2)]
        ybo = [TB(ar.alloc([T], BF16, "ybo")) for _ in range(2)]
        p.dma(wg.ap, w_glu[l].rearrange("(c q) n -> q c n", q=128), writes=[wg.b], q="pool")
        pi = 0
        for jo in range(8):
            zb, yo = zbs[jo % 2], ybo[jo % 2]
            p.dma(zb.ap, ZB.ap[jo * 128:(jo + 1) * 128, :], reads=[ZB.b], writes=[zb.b])
            bcol = bglu_sb.ap[:, l * 8 + jo:l * 8 + jo + 1]
            for tb in range(4):
                ps = PS[4 + pi % 4]
                sg = gsg[pi % 2]
                pi += 1
                ts_ = slice(tb * 512, (tb + 1) * 512)
                for kc in range(8):
                    p.op("pe", lambda e, ps=ps, kc=kc, jo=jo, ts_=ts_: e.matmul(ps.ap, lhsT=wg.ap[:, kc, jo * 128:(jo + 1) * 128], rhs=YS.ap[:, kc, ts_],
                                                                               start=(kc == 0), stop=(kc == 7)), reads=[wg.b, YS.b], writes=[ps.b])
                p.op("act", lambda e, ps=ps, sg=sg, bcol=bcol: e.activation(out=sg.ap, in_=ps.ap, func=AF.Sigmoid, bias=bcol),
                     reads=[ps.b, bglu_sb.b], writes=[sg.b])
                p.op("dve", lambda e, sg=sg, jo=jo, ts_=ts_: e.tensor_tensor(out=sg.ap, in0=sg.ap, in1=YS.ap[:, jo, ts_], op=ALU.mult),
                     reads=[sg.b, YS.b], writes=[sg.b])
                p.op("pool", lambda e, sg=sg, zb=zb, yo=yo, ts_=ts_: e.tensor_tensor(out=yo.ap[:, ts_], in0=sg.ap, in1=zb.ap[:, ts_], op=ALU.mult),
                     reads=[sg.b, zb.b], writes=[yo.b])
            p.dma(YBZ.ap[jo * 128:(jo + 1) * 128, :], yo.ap, reads=[yo.b], writes=[YBZ.b])

    def phase_p5(l):
        ar.reset()
        OAZs = TB(ar.alloc([8, T], BF16, "OAZs"))
        YBZs = TB(ar.alloc([8, T], BF16, "YBZs"))
        p.dma(OAZs.ap, OAZ.ap.rearrange("(c q) t -> q c t", q=128), reads=[OAZ.b], writes=[OAZs.b])
        p.dma(YBZs.ap, YBZ.ap.rearrange("(c q) t -> q c t", q=128), reads=[YBZ.b], writes=[YBZs.b])
        wpa_b = [TB(ar.alloc([8, 512], BF16, "wpa")) for _ in range(2)]
        wpb_b = [TB(ar.alloc([8, 512], BF16, "wpb")) for _ in range(2)]
        gas = [TB(ar.alloc([T], BF16, "ga")) for _ in range(2)]
        gbs = [TB(ar.alloc([T], BF16, "gb")) for _ in range(2)]
        t1s = [TB(ar.alloc([512], F32, "t1")) for _ in range(2)]
        t2s = [TB(ar.alloc([512], F32, "t2")) for _ in range(2)]
        mos = [TB(ar.alloc([T], BF16, "mo")) for _ in range(2)]
        wpa_l = w_pa[l].rearrange("(c q) n -> q c n", q=128)
        wpb_l = w_pb[l].rearrange("(c q) n -> q c n", q=128)
        pi = 0
        for jb in range(4):
            wa, wb_ = wpa_b[jb % 2], wpb_b[jb % 2]
            p.dma(wa.ap, wpa_l[:, :, jb * 512:(jb + 1) * 512], writes=[wa.b], q="pool")
            p.dma(wb_.ap, wpb_l[:, :, jb * 512:(jb + 1) * 512], writes=[wb_.b], q="pool")
            for sb_ in range(4):
                j = jb * 4 + sb_
                ga, gb, mo = gas[j % 2], gbs[j % 2], mos[j % 2]
                p.dma(ga.ap, GT.ap[j * 128:(j + 1) * 128, :], reads=[GT.b], writes=[ga.b])
                p.dma(gb.ap, GT.ap[2048 + j * 128:2048 + (j + 1) * 128, :], reads=[GT.b], writes=[gb.b])
                for tb in range(4):
                    pa_, pb_ = PS[(2 * pi) % 8], PS[(2 * pi + 1) % 8]
                    t1, t2 = t1s[pi % 2], t2s[pi % 2]
                    pi += 1
                    ts_ = slice(tb * 512, (tb + 1) * 512)
                    for kc in range(8):
                        p.op("pe", lambda e, pa_=pa_, wa=wa, kc=kc, sb_=sb_, ts_=ts_: e.matmul(
                            pa_.ap, lhsT=wa.ap[:, kc, sb_ * 128:(sb_ + 1) * 128], rhs=OAZs.ap[:, kc, ts_], start=(kc == 0), stop=(kc == 7)),
                            reads=[wa.b, OAZs.b], writes=[pa_.b])
                    for kc in range(8):
                        p.op("pe", lambda e, pb_=pb_, wb_=wb_, kc=kc, sb_=sb_, ts_=ts_: e.matmul(
                            pb_.ap, lhsT=wb_.ap[:, kc, sb_ * 128:(sb_ + 1) * 128], rhs=YBZs.ap[:, kc, ts_], start=(kc == 0), stop=(kc == 7)),
                            reads=[wb_.b, YBZs.b], writes=[pb_.b])
                    p.op("dve", lambda e, t1=t1, pa_=pa_, ga=ga, ts_=ts_: e.tensor_tensor(out=t1.ap, in0=pa_.ap, in1=ga.ap[:, ts_], op=ALU.mult),
                         reads=[pa_.b, ga.b], writes=[t1.b])
                    p.op("dve", lambda e, t2=t2, pb_=pb_, gb=gb, ts_=ts_: e.tensor_tensor(out=t2.ap, in0=pb_.ap, in1=gb.ap[:, ts_], op=ALU.mult),
                         reads=[pb_.b, gb.b], writes=[t2.b])
                    p.op("pool", lambda e, t1=t1, t2=t2, mo=mo, ts_=ts_: e.tensor_tensor(out=mo.ap[:, ts_], in0=t1.ap, in1=t2.ap, op=ALU.add),
                         reads=[t1.b, t2.b], writes=[mo.b])
                p.dma(MRG.ap[j * 128:(j + 1) * 128, :], mo.ap, reads=[mo.b], writes=[MRG.b])
        p.barrier()
        ar.reset()
        MRGs = TB(ar.alloc([16, T], BF16, "MRGs"))
        wout = TB(ar.alloc([16, D], BF16, "wout"))
        xts = [TB(ar.alloc([D], F32, "xt")) for _ in range(2)]
        xos = [TB(ar.alloc([D], F32, "xo")) for _ in range(2)]
        mv = MRG.ap.rearrange("(c q) t -> q c t", q=128)
        for hf in range(2):
            p.dma(MRGs.ap[:, hf * 8:(hf + 1) * 8, :], mv[:, hf * 8:(hf + 1) * 8, :], reads=[MRG.b], writes=[MRGs.b])
        wo_l = w_out[l].rearrange("(c q) n -> q c n", q=128)
        for nb in range(4):
            p.dma(wout.ap[:, :, nb * 512:(nb + 1) * 512], wo_l[:, :, nb * 512:(nb + 1) * 512], writes=[wout.b], q="pool")
        xsrc = x_in if l == 0 else XRES.ap
        xsrc_b = [] if l == 0 else [XRES.b]
        pi = 0
        for r in range(NT):
            xt, xo = xts[r % 2], xos[r % 2]
            p.dma(xt.ap, xsrc[r * 128:(r + 1) * 128, :], reads=xsrc_b, writes=[xt.b])
            for nb in range(4):
                ps = PS[pi % 8]
                pi += 1
                ns = slice(nb * 512, (nb + 1) * 512)
                for kc in range(16):
                    p.op("pe", lambda e, ps=ps, kc=kc, r=r, ns=ns: e.matmul(
                        ps.ap, lhsT=MRGs.ap[:, kc, r * 128:(r + 1) * 128], rhs=wout.ap[:, kc, ns], start=(kc == 0), stop=(kc == 15)),
                        reads=[MRGs.b, wout.b], writes=[ps.b])
                p.op("dve", lambda e, ps=ps, xt=xt, xo=xo, ns=ns: e.tensor_tensor(out=xo.ap[:, ns], in0=ps.ap, in1=xt.ap[:, ns], op=ALU.add),
                     reads=[ps.b, xt.b], writes=[xo.b])
            p.dma(XRES.ap[r * 128:(r + 1) * 128, :], xo.ap, reads=[xo.b], writes=[XRES.b])

    def phase_final():
        ar.reset()
        fgb = TB(ar.alloc([D], F32, "fgb"))
        xts = [TB(ar.alloc([D], F32, "xt")) for _ in range(2)]
        yos = [TB(ar.alloc([D], F32, "yo")) for _ in range(2)]
        junk = TB(ar.alloc([D], BF16, "junk"))
        ssq = [TB(ar.alloc([1], F32, "ss")) for _ in range(2)]
        p.dma(fgb.ap, final_g.partition_broadcast(128), writes=[fgb.b])
        for r in range(NT):
            xt, yo, ss = xts[r % 2], yos[r % 2], ssq[r % 2]
            p.dma(xt.ap, XRES.ap[r * 128:(r + 1) * 128, :], reads=[XRES.b], writes=[xt.b])
            p.op("act", lambda e, xt=xt, ss=ss: e.activation(out=junk.ap, in_=xt.ap, func=AF.Square, accum_out=ss.ap),
                 reads=[xt.b], writes=[junk.b, ss.b])
            p.op("act", lambda e, ss=ss: e.activation(out=ss.ap, in_=ss.ap, func=AF.Sqrt, scale=1.0 / D, bias=EPS), reads=[ss.b], writes=[ss.b])
            p.op("dve", lambda e, ss=ss: e.reciprocal(out=ss.ap, in_=ss.ap), reads=[ss.b], writes=[ss.b])
            p.op("dve", lambda e, xt=xt, ss=ss, yo=yo: e.scalar_tensor_tensor(out=yo.ap, in0=xt.ap, scalar=ss.ap, in1=fgb.ap,
                                                                               op0=ALU.mult, op1=ALU.mult),
                 reads=[xt.b, ss.b, fgb.b], writes=[yo.b])
            p.dma(out_d[r * 128:(r + 1) * 128, :], yo.ap, reads=[yo.b], writes=[OUTB])

    for l in range(nlayers):
        ar.reset()
        hT = TB(ar.alloc([16, T], BF16, "hT"))
        markA = ar.off
        lngb = TB(ar.alloc([D], F32, "lngb"))
        xts = [TB(ar.alloc([D], F32, "xt")) for _ in range(2)]
        junk = TB(ar.alloc([D], BF16, "junk"))
        hbs = [TB(ar.alloc([D], BF16, "hb")) for _ in range(2)]
        ssq = [TB(ar.alloc([1], F32, "ss")) for _ in range(2)]
        rsq = [TB(ar.alloc([1], F32, "rs")) for _ in range(2)]
        xsrc = x_in if l == 0 else XRES.ap
        xsrc_b = [] if l == 0 else [XRES.b]
        p.dma(lngb.ap, ln_g[l].partition_broadcast(128), writes=[lngb.b])
        for r in range(NT):
            xt, hb, ss, rs = xts[r % 2], hbs[r % 2], ssq[r % 2], rsq[r % 2]
            p.dma(xt.ap, xsrc[r * 128:(r + 1) * 128, :], reads=xsrc_b, writes=[xt.b])
            p.op("act", lambda e, xt=xt, ss=ss: e.activation(out=junk.ap, in_=xt.ap, func=AF.Square, accum_out=ss.ap),
                 reads=[xt.b], writes=[junk.b, ss.b])
            p.op("act", lambda e, ss=ss, rs=rs: e.activation(out=rs.ap, in_=ss.ap, func=AF.Sqrt, scale=1.0 / D, bias=EPS),
                 reads=[ss.b], writes=[rs.b])
            p.op("dve", lambda e, rs=rs: e.reciprocal(out=rs.ap, in_=rs.ap), reads=[rs.b], writes=[rs.b])
            p.op("dve", lambda e, xt=xt, rs=rs, hb=hb: e.scalar_tensor_tensor(out=hb.ap, in0=xt.ap, scalar=rs.ap, in1=lngb.ap,
                                                                               op0=ALU.mult, op1=ALU.mult),
                 reads=[xt.b, rs.b, lngb.b], writes=[hb.b])
            for half in range(2):
                pt = PS[(2 * r + half) % 2]
                ptv = pt.ap.bitcast(BF16).rearrange("p (c t) -> p c t", c=8)
                for c in range(8):
                    cc = half * 8 + c
                    p.op("pe", lambda e, c=c, cc=cc, hb=hb, ptv=ptv: e.transpose(out=ptv[:, c, :], in_=hb.ap[:, cc * 128:(cc + 1) * 128],
                                                                                  identity=ident_b.ap),
                         reads=[hb.b, ident_b.b], writes=[pt.b])
                dst = hT.ap[:, half * 8:(half + 1) * 8, r * 128:(r + 1) * 128]
                if half == 0:
                    p.op("dve", lambda e, dst=dst, ptv=ptv: e.tensor_copy(out=dst, in_=ptv), reads=[pt.b], writes=[hT.b])
                else:
                    p.op("act", lambda e, dst=dst, ptv=ptv: e.activation(out=dst, in_=ptv, func=AF.Copy), reads=[pt.b], writes=[hT.b])

        blocks = []
        for j in range(6):
            blocks.append((QKV, j * 512, j * 512, AF.Copy, None))
        for j in range(2):
            blocks.append((ZA, j * 512, 3072 + j * 512, AF.Silu, None))
        for j in range(2):
            blocks.append((UU, j * 512, 4128 + j * 512, AF.Copy, None))
        for j in range(2):
            blocks.append((ZB, j * 512, 5152 + j * 512, AF.Silu, None))
        for j in range(8):
            blocks.append((GT, j * 512, 6176 + j * 512, AF.Sigmoid, l * 32 + j * 4))
        w_l = w_in[l].rearrange("(c q) n -> q c n", q=128)
        wbs = [TB(ar.alloc([16, 512], BF16, "wb")) for _ in range(2)]
        obs = [TB(ar.alloc([T], BF16, "ob")) for _ in range(2)]
        ba_sb = TB(ar.alloc([16, 32], F32, "ba"))
        bi = 0
        gi = 0
        oi = 0
        for (dst, row0, col0, func, bcol) in blocks:
            wb = wbs[bi % 2]
            bi += 1
            p.dma(wb.ap, w_l[:, :, col0:col0 + 512], writes=[wb.b], q="pool")
            for sub in range(4):
                ob = obs[oi % 2]
                oi += 1
                pg = [PS[(gi % 2) * 4 + tb] for tb in range(4)]
                gi += 1
                for c in range(16):
                    for tb in range(4):
                        ps = pg[tb]
                        p.op("pe", lambda e, ps=ps, wb=wb, c=c, sub=sub, tb=tb: e.matmul(
                            ps.ap, lhsT=wb.ap[:, c, sub * 128:(sub + 1) * 128], rhs=hT.ap[:, c, tb * 512:(tb + 1) * 512],
                            start=(c == 0), stop=(c == 15)), reads=[wb.b, hT.b], writes=[ps.b])
                for tb in range(4):
                    ps = pg[tb]
                    if bcol is None:
                        p.op("act", lambda e, ob=ob, ps=ps, tb=tb, func=func: e.activation(
                            out=ob.ap[:, tb * 512:(tb + 1) * 512], in_=ps.ap, func=func), reads=[ps.b], writes=[ob.b])
                    else:
                        bias = bg_sb.ap[:, bcol + sub:bcol + sub + 1]
                        p.op("act", lambda e, ob=ob, ps=ps, tb=tb, func=func, bias=bias: e.activation(
                            out=ob.ap[:, tb * 512:(tb + 1) * 512], in_=ps.ap, func=func, bias=bias),
                            reads=[ps.b, bg_sb.b], writes=[ob.b])
                r0 = row0 + sub * 128
                p.dma(dst.ap[r0:r0 + 128, :], ob.ap, reads=[ob.b], writes=[dst.b])
        wba = TB(ar.alloc([16, 32], BF16, "wba"))
        p.dma(wba.ap, w_l[:, :, 4096:4128], writes=[wba.b], q="pool")
        psba = PS[6]
        pbv = psba.ap.rearrange("p (r n) -> p r n", r=16)
        for r in range(NT):
            for c in range(16):
                p.op("pe", lambda e, r=r, c=c: e.matmul(pbv[:, r, :], lhsT=hT.ap[:, c, r * 128:(r + 1) * 128], rhs=wba.ap[:, c, :],
                                                         start=(c == 0), stop=(c == 15)), reads=[hT.b, wba.b], writes=[psba.b])
        p.op("dve", lambda e: e.tensor_copy(out=ba_sb.ap, in_=pbv), reads=[psba.b], writes=[ba_sb.b])
        BAD = dbg.get("BA") or dscr(f"BA{l}", [128, 16, 32], F32)
        p.dma(BAD.ap, ba_sb.ap, reads=[ba_sb.b], writes=[BAD.b])
        p.barrier()
        if stop_after == ("P1", l):
            break
        phase_p2(l)
        p.barrier()
        if stop_after == ("P2", l):
            break
        phase_p3(l, BAD)
        p.barrier()
        if stop_after == ("P3", l):
            break
        phase_p4(l)
        p.barrier()
        if stop_after == ("P4", l):
            break
        phase_p5(l)
        p.new_epoch()
        if stop_after == ("P5", l):
            break
    else:
        phase_final()
    p.barrier()
    p.build()
    return nc


def make_in_map(inputs, b):
    f = lambda a: np.ascontiguousarray(a, dtype=np.float32)
    m = {"x": f(inputs["x"][b])}
    for k in ["ln_g", "w_in", "conv_w", "head_norm_g", "b_re", "b_im", "c_re", "c_im", "d_skip", "w_glu", "b_glu",
              "w_pa", "w_pb", "b_gate", "w_out", "final_g"]:
        m[k] = f(inputs[k])
    m["a_log"] = f(inputs["a_log"]).reshape(DEPTH, 16)
    m["dt_bias"] = f(inputs["dt_bias"]).reshape(DEPTH, 16)
    m["lam_re"] = f(inputs["lam_re"]).reshape(DEPTH, 128, 64)
    m["lam_im"] = f(inputs["lam_im"]).reshape(DEPTH, 128, 64)
    m["log_dt"] = f(inputs["log_dt"]).reshape(DEPTH, 128)
    return m


_NC_CACHE = {}


def kernel(**inputs):
    if "nc" not in _NC_CACHE:
        _NC_CACHE["nc"] = build_program()
    nc = _NC_CACHE["nc"]
    nb = inputs["x"].shape[0]
    maps = [make_in_map(inputs, b) for b in range(nb)]
    in_maps = [maps[c % nb] for c in range(NCORES)]
    res = run_bass_kernel_spmd(nc, in_maps, core_ids=list(range(NCORES)))
    out = np.stack([np.asarray(res.results[b]["out"], dtype=np.float32) for b in range(nb)], 0)
    return out
```
